# Optimizing a Trainium2 kernel written in Bass

```python
import jax, jax.numpy as jnp
from jax import lax
import numpy as np

D_MODEL = 1024
BATCH = 8
SEQ = 2048
DEPTH = 4
DEC_BATCH = 128
DEC_SEQ = 1
PAST_LEN = 16384
PAGE_SIZE = 128

N_MIXERS = 2
N_POOL_LAYERS = (DEPTH + 1) // 2
N_GMLP_LAYERS = DEPTH // 2
POOL_WINDOWS = (2, 4, 8, 16)
POOL_GROUP = D_MODEL // len(POOL_WINDOWS)
POOL_BUF = max(POOL_WINDOWS) - 1
CHUNK = 128
GMLP_GROUPS = 8
GMLP_WIDTH = D_MODEL
GMLP_GC = GMLP_WIDTH // GMLP_GROUPS
D_FF = ((8 * D_MODEL // 3 + 255) // 256) * 256
EPS = 1e-6

kernel_name = "hybrid_pool_gmlp_adaln_decoder_step"


def _rmsnorm(x, g):
    xf = x.astype(jnp.float32)
    y = xf * lax.rsqrt(jnp.mean(xf * xf, axis=-1, keepdims=True) + EPS)
    return (y * g.astype(jnp.float32)).astype(x.dtype)


def _layernorm(x, g, b):
    xf = x.astype(jnp.float32)
    mu = jnp.mean(xf, axis=-1, keepdims=True)
    var = jnp.mean(jnp.square(xf - mu), axis=-1, keepdims=True)
    y = (xf - mu) * lax.rsqrt(var + EPS)
    return (y * g.astype(jnp.float32) + b.astype(jnp.float32)).astype(x.dtype)


def _adaln(c, w, b):
    m = jax.nn.silu(c) @ w + b
    return jnp.split(m[:, None, :], 6, axis=-1)


def _pool_mixer(h_ext, valid_ext, w_grp, scale, w_out):
    B, L, D = h_ext.shape
    T = L - POOL_BUF
    hf = h_ext.astype(jnp.float32)
    cs = jnp.pad(jnp.cumsum(hf, axis=1), ((0, 0), (1, 0), (0, 0)))
    cv = jnp.pad(jnp.cumsum(valid_ext.astype(jnp.float32), axis=1), ((0, 0), (1, 0)))
    groups = []
    for g, w in enumerate(POOL_WINDOWS):
        lo, hi = g * POOL_GROUP, (g + 1) * POOL_GROUP
        s = cs[:, POOL_BUF + 1:, lo:hi] - cs[:, POOL_BUF + 1 - w:L + 1 - w, lo:hi]
        n = cv[:, POOL_BUF + 1:] - cv[:, POOL_BUF + 1 - w:L + 1 - w]
        groups.append(s / n[..., None])
    pooled = jnp.concatenate(groups, axis=-1) - hf[:, POOL_BUF:]
    pooled = pooled.reshape(B, T, len(POOL_WINDOWS), POOL_GROUP)
    mixed = jnp.einsum('btgc,gcd->btgd', pooled, w_grp.astype(jnp.float32))
    mixed = mixed.reshape(B, T, D) * scale.astype(jnp.float32)
    return mixed.astype(h_ext.dtype) @ w_out


def _gmlp_mixer(h, w_in, ln_g, ln_b, w_s, b_s, w_out):
    B, T, D = h.shape
    uv = h @ w_in
    u, v = uv[..., :GMLP_WIDTH], uv[..., GMLP_WIDTH:]
    v = _layernorm(v, ln_g, ln_b)
    Lc = CHUNK if T >= CHUNK else T
    n_chunks = T // Lc
    vc = v.reshape(B, n_chunks, Lc, GMLP_GROUPS, GMLP_GC)
    ws = jnp.tril(w_s[:, :Lc, :Lc])
    mixed = jnp.einsum('gts,bcsgd->bctgd', ws, vc) + b_s[:, :Lc].T[None, None, :, :, None]
    out = u * mixed.reshape(B, T, GMLP_WIDTH)
    return out @ w_out, v


def _run_group(x, c, pool_hist, pool_valid,
               w_ada, b_ada, norm_mix, norm_ffn, norm_final,
               pool_w_grp, pool_scale, pool_w_out,
               gmlp_w_in, gmlp_ln_g, gmlp_ln_b, gmlp_w_s, gmlp_b_s, gmlp_w_out,
               ffn_w_gate, ffn_w_up, ffn_w_down):
    new_pool, new_v = [], []
    for i in range(DEPTH):
        sh1, sc1, g1, sh2, sc2, g2 = _adaln(c, w_ada[i], b_ada[i])
        h = _rmsnorm(x, norm_mix[i]) * (1 + sc1) + sh1
        if i % N_MIXERS == 0:
            j = i // N_MIXERS
            h_ext = jnp.concatenate([pool_hist[j].astype(h.dtype), h], axis=1)
            out = _pool_mixer(h_ext, pool_valid, pool_w_grp[j], pool_scale[j], pool_w_out[j])
            new_pool.append(h_ext[:, -POOL_BUF:])
        else:
            j = i // N_MIXERS
            out, v = _gmlp_mixer(h, gmlp_w_in[j], gmlp_ln_g[j], gmlp_ln_b[j],
                                 gmlp_w_s[j], gmlp_b_s[j], gmlp_w_out[j])
            new_v.append(v)
        x = x + g1 * out
        h = _rmsnorm(x, norm_ffn[i]) * (1 + sc2) + sh2
        ff = (jax.nn.silu(h @ ffn_w_gate[i]) * (h @ ffn_w_up[i])) @ ffn_w_down[i]
        x = x + g2 * ff
    return _rmsnorm(x, norm_final), jnp.stack(new_pool), jnp.stack(new_v)


def setup_inputs(seed: int = 0) -> dict:
    key = jax.random.key(seed)
    ks = jax.random.split(key, 24)
    D, F = D_MODEL, D_FF
    nrm = lambda k, shape, s: jax.random.normal(k, shape, jnp.float32) * s
    return {
        "x_prompt": nrm(ks[0], (BATCH, SEQ, D), 1.0),
        "x_sample": nrm(ks[1], (DEC_BATCH, DEC_SEQ, D), 1.0),
        "state_pool": nrm(ks[2], (N_POOL_LAYERS, DEC_BATCH, POOL_BUF, D), 1.0),
        "c_prompt": nrm(ks[3], (BATCH, D), 1.0),
        "c_sample": nrm(ks[4], (DEC_BATCH, D), 1.0),
        "w_ada": nrm(ks[5], (DEPTH, D, 6 * D), 0.5 * D ** -0.5),
        "b_ada": nrm(ks[6], (DEPTH, 6 * D), 0.02),
        "norm_mix": 1.0 + nrm(ks[7], (DEPTH, D), 0.05),
        "norm_ffn": 1.0 + nrm(ks[8], (DEPTH, D), 0.05),
        "norm_final": 1.0 + nrm(ks[9], (D,), 0.05),
        "pool_w_grp": nrm(ks[10], (N_POOL_LAYERS, len(POOL_WINDOWS), POOL_GROUP, POOL_GROUP), POOL_GROUP ** -0.5),
        "pool_scale": 1.0 + nrm(ks[11], (N_POOL_LAYERS, D), 0.1),
        "pool_w_out": nrm(ks[12], (N_POOL_LAYERS, D, D), D ** -0.5),
        "gmlp_w_in": nrm(ks[13], (N_GMLP_LAYERS, D, 2 * GMLP_WIDTH), D ** -0.5),
        "gmlp_ln_g": 1.0 + nrm(ks[14], (N_GMLP_LAYERS, GMLP_WIDTH), 0.05),
        "gmlp_ln_b": nrm(ks[15], (N_GMLP_LAYERS, GMLP_WIDTH), 0.02),
        "gmlp_w_s": nrm(ks[16], (N_GMLP_LAYERS, GMLP_GROUPS, CHUNK, CHUNK), CHUNK ** -0.5),
        "gmlp_b_s": 1.0 + nrm(ks[17], (N_GMLP_LAYERS, GMLP_GROUPS, CHUNK), 0.05),
        "gmlp_w_out": nrm(ks[18], (N_GMLP_LAYERS, GMLP_WIDTH, D), GMLP_WIDTH ** -0.5),
        "ffn_w_gate": nrm(ks[19], (DEPTH, D, F), D ** -0.5),
        "ffn_w_up": nrm(ks[20], (DEPTH, D, F), D ** -0.5),
        "ffn_w_down": nrm(ks[21], (DEPTH, F, D), F ** -0.5),
    }


def reference(x_prompt, x_sample, state_pool, c_prompt, c_sample,
              w_ada, b_ada, norm_mix, norm_ffn, norm_final,
              pool_w_grp, pool_scale, pool_w_out,
              gmlp_w_in, gmlp_ln_g, gmlp_ln_b, gmlp_w_s, gmlp_b_s, gmlp_w_out,
              ffn_w_gate, ffn_w_up, ffn_w_down):
    weights = (w_ada, b_ada, norm_mix, norm_ffn, norm_final,
               pool_w_grp, pool_scale, pool_w_out,
               gmlp_w_in, gmlp_ln_g, gmlp_ln_b, gmlp_w_s, gmlp_b_s, gmlp_w_out,
               ffn_w_gate, ffn_w_up, ffn_w_down)
    B, S, D = x_prompt.shape
    hist_p = jnp.zeros((N_POOL_LAYERS, B, POOL_BUF, D), x_prompt.dtype)
    valid_p = jnp.concatenate([jnp.zeros((1, POOL_BUF), jnp.float32),
                               jnp.ones((1, S), jnp.float32)], axis=1)
    y_prompt, new_pool_prompt, _ = _run_group(x_prompt, c_prompt, hist_p, valid_p, *weights)
    T = x_sample.shape[1]
    valid_s = jnp.ones((1, POOL_BUF + T), jnp.float32)
    y_sample, new_pool_sample, new_chunk_v_sample = _run_group(
        x_sample, c_sample, state_pool, valid_s, *weights)
    return (y_prompt, y_sample, new_pool_prompt, new_pool_sample, new_chunk_v_sample)
```

```python
import numpy as np
from contextlib import ExitStack

import concourse.bass as bass
import concourse.mybir as mybir
from concourse.bass_utils import run_bass_kernel_spmd

F32 = mybir.dt.float32
BF16 = mybir.dt.bfloat16
AF = mybir.ActivationFunctionType
ALU = mybir.AluOpType

D = 1024
NCH = 8
FF = 2816
NFT = 11
WINDOWS = (2, 4, 8, 16)
PB = 15
EPS = 1e-6
NS = 16
NCORES = 8
NSLOT = 8
NCST = 480

ENGS = ("tensor", "scalar", "vector", "gpsimd", "sync")


class Buf:
    __slots__ = ("key", "w", "r")

    def __init__(self, key):
        self.key = key
        self.w = {}
        self.r = {}


class Op:
    __slots__ = ("fn", "deps", "dma")

    def __init__(self, fn, deps, dma):
        self.fn = fn
        self.deps = deps
        self.dma = dma


class Plan:
    def __init__(self):
        self.ops = {e: [] for e in ENGS}
        self.dmacnt = {}
        self.bufs = {}
        self.final_sems = set()

    def buf(self, *key):
        b = self.bufs.get(key)
        if b is None:
            b = self.bufs[key] = Buf(key)
        return b

    def op(self, eng, fn, r=(), w=(), dma=None):
        idx = len(self.ops[eng])
        deps = {}

        def add(tok):
            k = (tok[0], tok[1])
            old = deps.get(k)
            if old is None or (tok[2] is not None and old[2] is not None and tok[2] > old[2]):
                deps[k] = tok

        is_dma = dma is not None
        for b in r:
            for tok in b.w.values():
                add(tok)
        for b in w:
            for tok in b.w.values():
                if is_dma and tok[0] == "d" and tok[1] == dma:
                    continue
                if is_dma or tok[0] == "d" or tok[1] != eng or eng != "tensor":
                    add(tok)
            for tok in b.r.values():
                if is_dma or tok[0] == "d" or tok[1] != eng or eng != "tensor":
                    add(tok)
        if is_dma:
            self.dmacnt[dma] = self.dmacnt.get(dma, 0) + 16
            tok = ("d", dma, None if dma in self.final_sems else self.dmacnt[dma])
        else:
            tok = ("e", eng, idx)
        for b in r:
            b.r[(tok[0], tok[1])] = tok
        for b in w:
            b.w[(tok[0], tok[1])] = tok
            b.r = {}
        self.ops[eng].append(Op(fn, list(deps.values()), dma))
        return tok

    def emit(self, nc, stack, final_waits):
        sig = {e: [False] * len(self.ops[e]) for e in ENGS}
        for e in ENGS:
            for o in self.ops[e]:
                for t in o.deps:
                    if t[0] == "e":
                        sig[t[1]][t[2]] = True
        cum = {}
        for e in ENGS:
            c = 0
            arr = []
            for i in range(len(self.ops[e])):
                if sig[e][i]:
                    c += 1
                arr.append(c)
            cum[e] = arr
        names = set()
        for e in ENGS:
            for i, o in enumerate(self.ops[e]):
                if o.dma is not None:
                    names.add("D_" + o.dma)
                elif sig[e][i]:
                    names.add("E_" + e)
        sems = {n: stack.enter_context(nc.semaphore(n)) for n in sorted(names)}
        stats = {}
        with nc.Block() as block:
            for e in ENGS:
                def body(engine, e=e):
                    waited = {}
                    nwait = 0
                    for i, o in enumerate(self.ops[e]):
                        for t in o.deps:
                            if t[0] == "e":
                                name = "E_" + t[1]
                                val = cum[t[1]][t[2]]
                            else:
                                name = "D_" + t[1]
                                val = t[2] if t[2] is not None else self.dmacnt[t[1]]
                            if waited.get(name, 0) < val:
                                engine.wait_ge(sems[name], val)
                                waited[name] = val
                                nwait += 1
                        ins = o.fn(engine)
                        if o.dma is not None:
                            ins.then_inc(sems["D_" + o.dma], 16)
                        elif sig[e][i]:
                            ins.then_inc(sems["E_" + e], 1)
                    if e == "sync":
                        for sname in final_waits:
                            if sname in self.dmacnt:
                                engine.wait_ge(sems["D_" + sname], self.dmacnt[sname])
                    stats[e] = (len(self.ops[e]), nwait)
                getattr(block, e)(body)
        return stats


class Cfg:
    def __init__(self, SEQ=2048, DEPTH=4):
        self.SEQ = SEQ
        self.DEPTH = DEPTH


def cv_layout(DEPTH):
    NPOOL = (DEPTH + 1) // 2
    NG = max(DEPTH // 2, 1)
    segs = [("b_ada", DEPTH * 48), ("norm_mix", DEPTH * 8), ("norm_ffn", DEPTH * 8),
            ("norm_final", 8), ("pool_scale", NPOOL * 8), ("gmlp_ln_g", NG * 8), ("gmlp_ln_b", NG * 8)]
    off = {}
    o = 0
    for n, k in segs:
        off[n] = o
        o += k
    return segs, off, o


def build_program(cfg):
    SEQ, DEPTH = cfg.SEQ, cfg.DEPTH
    NPB = SEQ // 512
    NTT = SEQ // 128
    TOK = SEQ + NS
    blocks = [(512 * b, 512) for b in range(NPB)] + [(SEQ, NS)]
    NB = len(blocks)
    SB = NB - 1
    NPOOL = (DEPTH + 1) // 2
    NG = DEPTH // 2
    NGA = max(NG, 1)

    nc = bass.Bass("TRN2", target_bir_lowering=False)

    def din(name, shape):
        return nc.dram_tensor(name, list(shape), F32, kind="ExternalInput").ap()

    def dout(name, shape):
        return nc.dram_tensor(name, list(shape), F32, kind="ExternalOutput").ap()

    xp = din("xp", [SEQ, D])
    xs = din("xs", [NS, D])
    sp = din("sp", [NPOOL, NS, PB, D])
    cc = din("cc", [1 + NS, D])
    cst = din("cst", [128, NCST])
    pmat_d = din("pmat", [128, 12 * 128])
    w_ada = din("w_ada", [DEPTH, D, 6 * D])
    b_ada = din("b_ada", [DEPTH, 6 * D])
    norm_mix = din("norm_mix", [DEPTH, D])
    norm_ffn = din("norm_ffn", [DEPTH, D])
    norm_final = din("norm_final", [D])
    pool_w_grp = din("pool_w_grp", [NPOOL, 4, 256, 256])
    pool_scale = din("pool_scale", [NPOOL, D])
    pool_w_out = din("pool_w_out", [NPOOL, D, D])
    gmlp_w_in = din("gmlp_w_in", [NGA, D, 2 * D])
    gmlp_ln_g = din("gmlp_ln_g", [NGA, D])
    gmlp_ln_b = din("gmlp_ln_b", [NGA, D])
    gmlp_w_s = din("gmlp_w_s", [NGA, 8, 128, 128])
    gmlp_b_s = din("gmlp_b_s", [NGA, 8, 128])
    gmlp_w_out = din("gmlp_w_out", [NGA, D, D])
    ffn_w_gate = din("ffn_w_gate", [DEPTH, D, FF])
    ffn_w_up = din("ffn_w_up", [DEPTH, D, FF])
    ffn_w_down = din("ffn_w_down", [DEPTH, FF, D])

    yp = dout("yp", [SEQ, D])
    ys = dout("ys", [NS, D])
    npp = dout("npp", [NPOOL, PB, D])
    nps = dout("nps", [NPOOL, NS, PB, D])
    ncv = dout("ncv", [NGA, NS, D])

    segs, cvoff, NCV = cv_layout(DEPTH)
    NCVT = (NCV + 127) // 128

    P = Plan()
    P.final_sems.add("init")
    st = ExitStack()

    def sb(name, shape, dt):
        return st.enter_context(nc.sbuf_tensor(name, list(shape), dt))

    xT = sb("xT", [128, 8, TOK], F32)
    hT = sb("hT", [128, 8, 16 + TOK], BF16)
    wk = sb("wk", [128, 8, TOK], BF16)
    wsl = sb("wsl", [128, NSLOT, 2048], BF16)
    mod = sb("mod", [128, 2, 48, 17], F32)
    Amod = sb("Amod", [128, 2, 2, 8, 17], F32)
    colv = sb("colv", [128, NCVT * 128], F32)
    cstt = sb("cstt", [128, NCST], F32)
    ones_m = sb("ones_m", [128, 128], BF16)
    ones_1 = sb("ones_1", [128, 128], BF16)
    scT = sb("scT", [128, 8, 17], BF16)
    rstd = sb("rstd", [128, 2, 512], F32)
    tmp = sb("tmp", [128, 3, 512], F32)
    sq = sb("sq", [128, 3, 512], BF16)
    hs32 = sb("hs32", [128, 8, 16], F32)
    hlast = sb("hlast", [128, 8, 16], F32)
    t16 = sb("t16", [128, 8, 16], F32)
    t16b = sb("t16b", [128, 2, 16], F32)
    small = sb("small", [128, 64], F32)
    ws00 = sb("ws00", [128, 8], F32)
    bs0 = sb("bs0", [128, 8], F32)
    vnT = sb("vnT", [128, 8, 16], F32)
    arena = sb("arena", [128, 3680], F32)
    epsc = sb("epsc", [128, 1], F32)
    pmat = sb("pmat_sb", [128, 12 * 128], BF16)
    pl16 = sb("pl16", [128, 8, 16], BF16)
    fz = sb("fz", [128, 1], F32)
    ps = st.enter_context(nc.psum_tensor("ps", [128, 8, 512], F32))

    R0a = arena[:, 0:512]
    R0b = arena[:, 512:1056]
    R0 = arena[:, 0:1024]
    R1 = arena[:, 1056:2080]
    R2 = arena[:, 2112:3136]
    R3 = arena[:, 3168:3680]
    bR0a, bR0b, bR1 = P.buf("R0a"), P.buf("R0b"), P.buf("R1")
    bR2a, bR2b, bR3 = P.buf("R2a"), P.buf("R2b"), P.buf("R3")
    bR0 = [bR0a, bR0b]
    bR2 = [bR2a, bR2b]

    ident = cstt[:, 0:128]
    maskT = cstt[:, 128:256]
    rc = cstt[:, 256:320]
    sel = cstt[:, 320:352]
    E0 = cstt[:, 352:480]
    bCST = P.buf("cst")
    bONES = P.buf("ones")
    bCOLV = P.buf("colv")
    bSCT = P.buf("scT")
    bPMAT = P.buf("pmat")
    bPL16 = P.buf("pl16")

    X = [[P.buf("x", c, b) for b in range(NB)] for c in range(8)]
    H = [[P.buf("h", c, b) for b in range(NB)] for c in range(8)]
    WK = [[P.buf("wk", c, b) for b in range(NB)] for c in range(8)]
    PSB = [P.buf("ps", k) for k in range(8)]
    SLOT = [P.buf("slot", s) for s in range(NSLOT)]
    MODB = [[P.buf("mod", p, m) for m in range(6)] for p in range(2)]
    SMALLB = [P.buf("small", q) for q in range(3)]
    AMOD = [[P.buf("amod", p, wh) for wh in range(2)] for p in range(2)]
    RSTD = [P.buf("rstd", q) for q in range(2)]
    TMP = [P.buf("tmp", q) for q in range(3)]
    SQ = [P.buf("sq", q) for q in range(3)]
    bHS32, bHLAST, bT16, bT16B = P.buf("hs32"), P.buf("hlast"), P.buf("t16"), P.buf("t16b")
    bSMALL, bWS00, bBS0, bVNT = P.buf("small"), P.buf("ws00"), P.buf("bs0"), P.buf("vnT")

    state = {"bank": 0, "rq": 0, "tq": 0, "sq": 0, "t16b": 0}
    free_slots = list(range(NSLOT))

    held_banks = set()

    def alloc_bank(n=1):
        p = state["bank"]
        for _ in range(32):
            if n == 2 and p % 2 == 1:
                p = (p + 1) % 8
            if p + n > 8:
                p = 0
            if all((p + k_) not in held_banks for k_ in range(n)):
                state["bank"] = (p + n) % 8
                return p
            p = (p + 1) % 8
        raise RuntimeError("no free PSUM bank")

    def nxt(key, n):
        v = state[key]
        state[key] = (v + 1) % n
        return v

    def slot_alloc():
        assert free_slots, "weight ring exhausted"
        return free_slots.pop(0)

    def wrel(s):
        assert s not in free_slots
        free_slots.append(s)

    def wget_cols(W2d, col0, ncols=256):
        s = slot_alloc()
        src = W2d[:, col0:col0 + ncols].rearrange("(k p) n -> p k n", p=128)
        dst = wsl[:, s, 0:8 * ncols].rearrange("p (k n) -> p k n", k=8)
        P.op("gpsimd", lambda e: e.dma_start(out=dst, in_=src), w=[SLOT[s]], dma="w%d" % s)
        return s, dst

    def wget_rows(W2d, row0):
        s = slot_alloc()
        src = W2d[row0:row0 + 256, :].rearrange("(j p) n -> p j n", p=128)
        dst = wsl[:, s, :].rearrange("p (j n) -> p j n", j=2)
        P.op("gpsimd", lambda e: e.dma_start(out=dst, in_=src), w=[SLOT[s]], dma="w%d" % s)
        return s, dst

    def cvcol(name, row):
        o = cvoff[name] + row
        return colv[:, o:o + 1]

    def mm(out, lhsT, rhs, start, stop, r, w):
        P.op("tensor", lambda e: e.matmul(out, lhsT, rhs, start=start, stop=stop), r=r, w=w)

    def tr(out, in_, idn, r, w):
        P.op("tensor", lambda e: e.transpose(out, in_, idn), r=r, w=w)

    def ACT(out, in_, func, r, w, bias=None, scale=None):
        kw = {}
        if bias is not None:
            kw["bias"] = bias
        if scale is not None:
            kw["scale"] = scale
        P.op("scalar", lambda e: e.activation(out=out, in_=in_, func=func, **kw), r=r, w=w)

    def ACOPY(out, in_, r, w):
        P.op("scalar", lambda e: e.copy(out=out, in_=in_), r=r, w=w)

    def VCOPY(out, in_, r, w):
        P.op("vector", lambda e: e.tensor_copy(out=out, in_=in_), r=r, w=w)

    def TT(out, in0, in1, op, r, w):
        P.op("vector", lambda e: e.tensor_tensor(out=out, in0=in0, in1=in1, op=op), r=r, w=w)

    def TS(out, in0, s1, s2, op0, op1, r, w):
        if op1 is None:
            P.op("vector", lambda e: e.tensor_scalar(out=out, in0=in0, scalar1=s1, scalar2=None, op0=op0), r=r, w=w)
        else:
            P.op("vector", lambda e: e.tensor_scalar(out=out, in0=in0, scalar1=s1, scalar2=s2, op0=op0, op1=op1),
                 r=r, w=w)

    def STT(out, in0, scalar, in1, op0, op1, r, w):
        P.op("vector", lambda e: e.scalar_tensor_tensor(out=out, in0=in0, scalar=scalar, in1=in1, op0=op0, op1=op1),
             r=r, w=w)

    def DMA(eng, out, in_, sem, r=(), w=()):
        P.op(eng, lambda e: e.dma_start(out=out, in_=in_), r=r, w=w, dma=sem)

    def MEMSET(ap, val, w):
        P.op("vector", lambda e: e.memset(ap, val), w=w)

    ZTH = [[P.buf("zt", zi, hf) for hf in range(2)] for zi in range(3)]
    ALIAS_R23 = [bR2a, bR2b, bR3] + [b_ for pr in ZTH for b_ in pr]

    def FENCE(bufs):
        P.op("vector", lambda e: e.memset(fz[:, :], 0.0), w=bufs)

    def rsqrt_act(out, in_, npart, r, w):
        ACT(out, in_, AF.Ln, r=list(r) + [bONES], w=w, bias=epsc[0:npart, :], scale=1.0)
        ACT(out, out, AF.Exp, r=w, w=w, scale=-0.5)

    def pipeline(items, order, per_step=None):
        n = len(order)
        max_off = 0
        for it in items:
            max_off = max(max_off, it[1])
            if len(it) > 2:
                for _, co_off in it[2]:
                    max_off = max(max_off, co_off)
        for step in range(n + max_off):
            if per_step is not None:
                per_step()
            for it in items:
                fn, off = it[0], it[1]
                jj = step - off
                has_fn = 0 <= jj < n
                if len(it) > 2:
                    act = [(cf, order[step - co]) for cf, co in it[2] if 0 <= step - co < n]

                    def co(c, act=act):
                        for cf, bb in act:
                            cf(bb, c)
                    if has_fn:
                        fn(order[jj], co if act else None)
                    elif act:
                        for c in range(-1, 8):
                            co(c)
                elif has_fn:
                    fn(order[jj])

    ORDER = [SB] + list(range(NPB))

    DMA("sync", cstt[:, :], cst[:, :], "init", w=[bCST])
    MEMSET(ones_m[:, :], 1.0 / 1024.0, [bONES])
    MEMSET(ones_1[:, :], 1.0, [bONES])
    MEMSET(epsc[:, :], EPS, [bONES])
    MEMSET(hT[:, :, 0:16], 0.0, [H[c][0] for c in range(8)])


    cvst = R3
    srcs = {
        "b_ada": b_ada.rearrange("l (r p) -> (l r) p", p=128),
        "norm_mix": norm_mix.rearrange("l (r p) -> (l r) p", p=128),
        "norm_ffn": norm_ffn.rearrange("l (r p) -> (l r) p", p=128),
        "norm_final": norm_final.rearrange("(r p) -> r p", p=128),
        "pool_scale": pool_scale.rearrange("l (r p) -> (l r) p", p=128),
        "gmlp_ln_g": gmlp_ln_g.rearrange("l (r p) -> (l r) p", p=128),
        "gmlp_ln_b": gmlp_ln_b.rearrange("l (r p) -> (l r) p", p=128),
    }
    assert NCVT <= 4
    for name, nrows in segs:
        o = cvoff[name]
        done = 0
        while done < nrows:
            t = (o + done) // 128
            p0 = (o + done) % 128
            n = min(nrows - done, 128 - p0)
            DMA("sync", cvst[p0:p0 + n, t * 128:(t + 1) * 128], srcs[name][done:done + n, :], "init", w=[bR3])
            done += n
    for t in range(NCVT):
        nrow = min(128, NCV - t * 128)
        bk = alloc_bank()
        tr(ps[:, bk, 0:nrow], cvst[0:nrow, t * 128:(t + 1) * 128], ident[0:nrow, 0:nrow],
           r=[bR3, bCST], w=[PSB[bk]])
        VCOPY(colv[:, t * 128:t * 128 + nrow], ps[:, bk, 0:nrow], r=[PSB[bk]], w=[bCOLV])

    DMA("sync", R2[0:1 + NS, :], cc[:, :], "init", w=bR2)
    bk = alloc_bank()
    for c in range(8):
        tr(ps[:, bk, c * 17:(c + 1) * 17], R2[0:17, c * 128:(c + 1) * 128], ident[0:17, 0:17],
           r=bR2 + [bCST], w=[PSB[bk]])
    ACT(scT[:, :, :], ps[:, bk, 0:136].rearrange("p (c n) -> p c n", c=8), AF.Silu, r=[PSB[bk]], w=[bSCT])

    ada_state = {"layer": 0, "next": 0}

    def ada_tile(i, t):
        par = i % 2
        m = t // 4
        s, wv = wget_cols(w_ada[i], t * 256)
        for oc in range(2):
            ch = 2 * t + oc
            bk = alloc_bank()
            for k in range(8):
                mm(ps[:, bk, 0:17], wv[:, k, oc * 128:(oc + 1) * 128], scT[:, k, :], k == 0, k == 7,
                   r=[SLOT[s], bSCT], w=[PSB[bk]])
            TS(mod[:, par, ch, :], ps[:, bk, 0:17], cvcol("b_ada", i * 48 + ch), None, ALU.add, None,
               r=[PSB[bk], bCOLV], w=[MODB[par][m]])
        wrel(s)
        if t % 4 == 3 and m in (1, 4):
            wh = 0 if m == 1 else 1
            nname = "norm_mix" if wh == 0 else "norm_ffn"
            for c in range(8):
                TS(Amod[:, par, wh, c, :], mod[:, par, m * 8 + c, :], 1.0, cvcol(nname, i * 8 + c),
                   ALU.add, ALU.mult, r=[MODB[par][m], bCOLV], w=[AMOD[par][wh]])

    def ada_pop(n=1):
        for _ in range(n):
            i, t = ada_state["layer"], ada_state["next"]
            if i >= DEPTH:
                return
            ada_tile(i, t)
            if t == 23:
                ada_state["layer"], ada_state["next"] = i + 1, 0
            else:
                ada_state["next"] = t + 1

    def ada_need(i, m):
        while (ada_state["layer"], ada_state["next"]) <= (i, 4 * m + 3) and ada_state["layer"] < DEPTH:
            ada_pop()

    def ada_allow(i):
        return ada_state["layer"] <= i

    stg = [R0, R1]
    bstg = [bR0, [bR1]]
    if TOK >= 2048:
        for c_ in range(4):
            stg.append(wk[:, c_, 0:2048].bitcast(F32))
            bstg.append([WK[c_][b_] for b_ in range(NPB)])
    NSTG = len(stg)
    xl = {"n": 0}

    def xload_block(b, co=None):
        if b == SB:
            q = xl["n"] % NSTG
            xl["n"] += 1
            DMA("sync", stg[q][0:NS, :], xs[:, :], "xin%d" % q, w=bstg[q])
            bk = alloc_bank()
            for c in range(8):
                tr(ps[:, bk, c * 16:(c + 1) * 16], stg[q][0:NS, c * 128:(c + 1) * 128], ident[0:NS, 0:NS],
                   r=bstg[q] + [bCST], w=[PSB[bk]])
            VCOPY(xT[:, :, SEQ:SEQ + NS], ps[:, bk, 0:128].rearrange("p (c n) -> p c n", c=8),
                  r=[PSB[bk]], w=[X[c][SB] for c in range(8)])
            if co is not None:
                for c in range(-1, 8):
                    co(c)
            return
        if co is not None:
            co(-1)
        for ti, tt in enumerate(range(4 * b, 4 * b + 4)):
            q = xl["n"] % NSTG
            xl["n"] += 1
            DMA("sync", stg[q], xp[tt * 128:(tt + 1) * 128, :], "xin%d" % q, w=bstg[q])
            bk = alloc_bank(2)
            for c in range(8):
                tr(ps[:, bk + c // 4, (c % 4) * 128:(c % 4 + 1) * 128], stg[q][:, c * 128:(c + 1) * 128],
                   ident, r=bstg[q] + [bCST], w=[PSB[bk + c // 4]])
            for hh in range(2):
                o_ap = xT[:, 4 * hh:4 * hh + 4, tt * 128:(tt + 1) * 128]
                i_ap = ps[:, bk + hh, :].rearrange("p (c n) -> p c n", c=4)
                wb = [X[c][b] for c in range(4 * hh, 4 * hh + 4)]
                if hh == 0:
                    ACOPY(o_ap, i_ap, r=[PSB[bk + hh]], w=wb)
                else:
                    VCOPY(o_ap, i_ap, r=[PSB[bk + hh]], w=wb)
            if co is not None:
                co(2 * ti)
                co(2 * ti + 1)

    nst = {}

    def norm_sq(key, b, c):
        col0, n = blocks[b]
        q = nxt("sq", 3)
        ACT(sq[:, q, 0:n], xT[:, c, col0:col0 + n], AF.Square, r=[X[c][b]], w=[SQ[q]])
        nst[(key, b, "q", c)] = q

    def norm_a(key, b, c):
        col0, n = blocks[b]
        if c == -1:
            bk = alloc_bank()
            nst[(key, b)] = bk
            held_banks.add(bk)
            norm_sq(key, b, 0)
            return
        bk = nst[(key, b)]
        q = nst.pop((key, b, "q", c))
        mm(ps[:, bk, 0:n], ones_m[:, :], sq[:, q, 0:n], c == 0, c == 7, r=[SQ[q], bONES], w=[PSB[bk]])
        if c < 7:
            norm_sq(key, b, c + 1)

    def norm_rstd(key, b):
        col0, n = blocks[b]
        bk = nst.pop((key, b))
        held_banks.discard(bk)
        rq = nxt("rq", 2)
        ACT(rstd[:, rq, 0:n], ps[:, bk, 0:n], AF.Ln, r=[PSB[bk], bONES], w=[RSTD[rq]], bias=epsc[:, :], scale=1.0)
        ACT(rstd[:, rq, 0:n], rstd[:, rq, 0:n], AF.Exp, r=[RSTD[rq]], w=[RSTD[rq]], scale=-0.5)
        return rq

    def norm_b(i, wh, pool_layer, b, c):
        par = i % 2
        col0, n = blocks[b]
        if c == -1:
            ada_need(i, 3 * wh + 1)
            nst[("rq", i, wh, b)] = norm_rstd((i, wh), b)
            return
        rq = nst[("rq", i, wh, b)]
        if c == 7:
            nst.pop(("rq", i, wh, b))
        MSH = MODB[par][3 * wh]
        tq = nxt("tq", 3)
        A = Amod[:, par, wh, c, :]
        sh = mod[:, par, (3 * wh) * 8 + c, :]
        if b != SB:
            STT(tmp[:, tq, 0:n], xT[:, c, col0:col0 + n], A[:, 0:1], rstd[:, rq, 0:n], ALU.mult, ALU.mult,
                r=[X[c][b], AMOD[par][wh], RSTD[rq]], w=[TMP[tq]])
            ACT(hT[:, c, 16 + col0:16 + col0 + n], tmp[:, tq, 0:n], AF.Identity,
                r=[TMP[tq], MSH], w=[H[c][b]], bias=sh[:, 0:1], scale=1.0)
            if pool_layer and wh == 0 and b == NPB - 1:
                ACT(hlast[:, c, :], tmp[:, tq, n - 16:n], AF.Identity,
                    r=[TMP[tq], MSH], w=[bHLAST], bias=sh[:, 0:1], scale=1.0)
        else:
            TT(tmp[:, tq, 0:NS], xT[:, c, col0:col0 + NS], rstd[:, rq, 0:NS], ALU.mult,
               r=[X[c][b], RSTD[rq]], w=[TMP[tq]])
            TT(tmp[:, tq, 0:NS], tmp[:, tq, 0:NS], A[:, 1:17], ALU.mult,
               r=[TMP[tq], AMOD[par][wh]], w=[TMP[tq]])
            TT(hs32[:, c, :], tmp[:, tq, 0:NS], sh[:, 1:17], ALU.add,
               r=[TMP[tq], MSH], w=[bHS32])
            ACOPY(hT[:, c, 16 + SEQ:16 + SEQ + NS], hs32[:, c, :], r=[bHS32], w=[H[c][b]])

    def x_update(bk, oc, b, par, m):
        col0, n = blocks[b]
        g = mod[:, par, m * 8 + oc, :]
        if b != SB:
            STT(xT[:, oc, col0:col0 + n], ps[:, bk, 0:n], g[:, 0:1], xT[:, oc, col0:col0 + n], ALU.mult, ALU.add,
                r=[PSB[bk], MODB[par][m], X[oc][b]], w=[X[oc][b]])
        else:
            tb = nxt("t16b", 2)
            TT(t16b[:, tb, :], ps[:, bk, 0:NS], g[:, 1:17], ALU.mult, r=[PSB[bk], MODB[par][m]], w=[bT16B])
            TT(xT[:, oc, col0:col0 + NS], t16b[:, tb, :], xT[:, oc, col0:col0 + NS], ALU.add,
               r=[bT16B, X[oc][b]], w=[X[oc][b]])

    def make_outproj(i, W2d, src, SRC, m):
        par = i % 2
        st_ = {"tiles": None, "done": 0}

        def stage(b, co=None):
            if st_["tiles"] is None:
                ada_need(i, m)
                st_["tiles"] = [wget_cols(W2d, t * 256) for t in range(4)]
            col0, n = blocks[b]
            if co is not None:
                co(-1)
            for oc in range(8):
                s, wv = st_["tiles"][oc // 2]
                oc2 = oc % 2
                bk = alloc_bank()
                for k in range(8):
                    mm(ps[:, bk, 0:n], wv[:, k, oc2 * 128:(oc2 + 1) * 128], src(k, col0, n), k == 0, k == 7,
                       r=[SLOT[s], SRC[k][b]], w=[PSB[bk]])
                if co is not None:
                    co(oc)
                x_update(bk, oc, b, par, m)
            st_["done"] += 1
            if st_["done"] == NB:
                for s, _ in st_["tiles"]:
                    wrel(s)
        return stage

    def tok_major_out(srcT, bsrc, dst_ap_fn, stage, bstage, sem):
        bk = alloc_bank(2)
        for c in range(8):
            tr(ps[0:16, bk + c // 4, (c % 4) * 128:(c % 4 + 1) * 128], srcT[:, c, :], ident,
               r=[bsrc, bCST], w=[PSB[bk + c // 4]])
        VCOPY(stage[0:16, :].rearrange("p (a n) -> p a n", a=2), ps[0:16, bk:bk + 2, :],
              r=[PSB[bk], PSB[bk + 1]], w=bstage)
        o_ap, i_ap = dst_ap_fn(stage)
        DMA("sync", o_ap, i_ap, sem, r=bstage)

    def make_pool(i):
        par = i % 2
        j = i // 2
        zt = [R2[:, 0:512].bitcast(BF16), R2[:, 512:1024].bitcast(BF16), R3.bitcast(BF16)]
        bzt = ZTH
        FENCE(ALIAS_R23)
        gst = {"slot": None, "gv": None, "done": 0, "zi": 0, "pend": None, "prev": None}

        def get_w():
            if gst["slot"] is None:
                s = slot_alloc()
                gv = wsl[:, s, :].rearrange("p (g k n) -> p g k n", g=4, k=2)
                for g in range(4):
                    DMA("gpsimd", gv[:, g, :, :], pool_w_grp[j, g].rearrange("(k p) n -> p k n", p=128),
                        "w%d" % s, w=[SLOT[s]])
                DMA("sync", R1, pool_scale[j].partition_broadcast(128), "pscale", w=[bR1])
                for k in range(2):
                    TT(gv[:, :, k, :], gv[:, :, k, :], R1.rearrange("p (g n) -> p g n", g=4), ALU.mult,
                       r=[SLOT[s], bR1], w=[SLOT[s]])
                gst["slot"], gst["gv"] = s, gv
            return gst["slot"], gst["gv"]

        def zA(tt):
            s, gv = get_w()
            b = tt // 4
            zi = gst["zi"] % 3
            gst["zi"] += 1
            bz = alloc_bank(2)
            for g in range(4):
                for k in range(2):
                    mm(ps[:, bz + g // 2, (g % 2) * 256:(g % 2 + 1) * 256],
                       hT[:, 2 * g + k, 16 + tt * 128:16 + (tt + 1) * 128], gv[:, g, k, :], k == 0, k == 1,
                       r=[SLOT[s], H[2 * g + k][b]], w=[PSB[bz + g // 2]])
            ACOPY(zt[zi][:, 0:512], ps[:, bz, :], r=[PSB[bz]], w=[bzt[zi][0]])
            VCOPY(zt[zi][:, 512:1024], ps[:, bz + 1, :], r=[PSB[bz + 1]], w=[bzt[zi][1]])
            return zi

        def zB(tt, zi, zprev):
            b = tt // 4
            for quad in range(2):
                bp = alloc_bank()
                for ci in range(4):
                    c = 4 * quad + ci
                    wi = c // 2
                    kind = 2 if tt == 0 else 0
                    pa = pmat[:, (kind * 4 + wi) * 128:(kind * 4 + wi + 1) * 128]
                    mm(ps[:, bp, ci * 128:(ci + 1) * 128], zt[zi][:, c * 128:(c + 1) * 128], pa, True, tt == 0,
                       r=[bzt[zi][quad], bPMAT], w=[PSB[bp]])
                    if tt > 0:
                        pb = pmat[:, (4 + wi) * 128:(4 + wi) * 128 + 16]
                        mm(ps[:, bp, ci * 128:ci * 128 + 16], zt[zprev][:, c * 128:(c + 1) * 128], pb, False, True,
                           r=[bzt[zprev][quad], bPMAT], w=[PSB[bp]])
                VCOPY(wk[:, 4 * quad:4 * quad + 4, tt * 128:(tt + 1) * 128],
                      ps[:, bp, :].rearrange("p (c n) -> p c n", c=4),
                      r=[PSB[bp]], w=[WK[c][b] for c in range(4 * quad, 4 * quad + 4)])

        HPRE = (TOK >= 2048)
        if HPRE:
            hbuf = [wk[:, 6 + t_, 0:2048].bitcast(F32) for t_ in range(2)]
            hb = [[WK[6 + t_][b_] for b_ in range(NB)] for t_ in range(2)]
            for t_ in range(2):
                DMA("sync", hbuf[t_][0:120, :], sp[j, 8 * t_:8 * t_ + 8].rearrange("b t d -> (b t) d"),
                    "hist%d" % t_, w=hb[t_])

        def zpool(b):
            if b == SB:
                s, gv = get_w()
                bk = alloc_bank()
                for tile in range(2):
                    if HPRE:
                        hsrc, hbufs = hbuf[tile], hb[tile]
                    else:
                        DMA("sync", R1[0:120, :], sp[j, 8 * tile:8 * tile + 8].rearrange("b t d -> (b t) d"), "hist",
                            w=[bR1])
                        hsrc, hbufs = R1, [bR1]
                    for c in range(8):
                        wi = c // 2
                        mm(ps[:, bk, c * 16 + 8 * tile:c * 16 + 8 * tile + 8], hsrc[0:120, c * 128:(c + 1) * 128],
                           sel[0:120, wi * 8:(wi + 1) * 8], True, True, r=hbufs + [bCST], w=[PSB[bk]])
                for c in range(8):
                    w = WINDOWS[c // 2]
                    TS(t16[:, c, :], hs32[:, c, :], (1.0 / w - 1.0), None, ALU.mult, None, r=[bHS32], w=[bT16])
                    STT(pl16[:, c, :], ps[:, bk, c * 16:(c + 1) * 16], 1.0 / w, t16[:, c, :],
                        ALU.mult, ALU.add, r=[PSB[bk], bT16], w=[bPL16])
                tok_major_out(hs32, bHS32, lambda stage: (nps[j, :, PB - 1, :], stage[0:16, :]), R1, [bR1], "out_ns")
                bk = alloc_bank()
                for g in range(4):
                    for oc in range(2):
                        ch = 2 * g + oc
                        for k in range(2):
                            mm(ps[:, bk, ch * 16:(ch + 1) * 16], gv[:, g, k, oc * 128:(oc + 1) * 128],
                               pl16[:, 2 * g + k, :], k == 0, k == 1, r=[SLOT[s], bPL16], w=[PSB[bk]])
                VCOPY(wk[:, :, SEQ:SEQ + NS], ps[:, bk, 0:128].rearrange("p (c n) -> p c n", c=8),
                      r=[PSB[bk]], w=[WK[c][SB] for c in range(8)])
            else:
                for tt in range(4 * b, 4 * b + 4):
                    zi = zA(tt)
                    if gst["pend"] is not None:
                        zB(*gst["pend"])
                    gst["pend"] = (tt, zi, gst["prev"])
                    gst["prev"] = zi
                if b == NPB - 1:
                    zB(*gst["pend"])
                    gst["pend"] = None
                    tok_major_out(hlast, bHLAST, lambda stage: (npp[j, :, :], stage[1:16, :]), R1, [bR1], "out_np")
            gst["done"] += 1
            if gst["done"] == NB:
                wrel(gst["slot"])

        outp = make_outproj(i, pool_w_out[j], lambda k, col0, n: wk[:, k, col0:col0 + n], WK, 2)
        get_w()
        return zpool, outp

    def gmlp_prep_parts(i):
        j = i // 2
        wss = R0.rearrange("p (g s) -> p g s", g=8)
        Cb = R1
        WsT = R3.bitcast(BF16).rearrange("p (g t) -> p g t", g=8)

        def part0():
            FENCE(ALIAS_R23)
            DMA("sync", wss, gmlp_w_s[j].rearrange("g t s -> t g s"), "wss", w=bR0)
            DMA("sync", Cb, gmlp_b_s[j].rearrange("g t -> (g t)").partition_broadcast(128), "cb", w=[bR1])

        def part1():
            VCOPY(bs0[:, :], R1[:, 0:1024:128], r=[bR1], w=[bBS0])
            bk = alloc_bank()
            mm(ps[:, bk, 0:8], E0, R0[:, 0:1024:128], True, True, r=bR0 + [bCST], w=[PSB[bk]])
            VCOPY(ws00[:, :], ps[:, bk, 0:8], r=[PSB[bk]], w=[bWS00])
            for half in range(2):
                bk = alloc_bank()
                for gl in range(4):
                    g = 4 * half + gl
                    tr(ps[:, bk, gl * 128:(gl + 1) * 128], wss[:, g, :], ident, r=bR0 + [bCST], w=[PSB[bk]])
                for gl in range(4):
                    g = 4 * half + gl
                    TT(WsT[:, g, :], ps[:, bk, gl * 128:(gl + 1) * 128], maskT, ALU.mult,
                       r=[PSB[bk], bCST], w=[bR3])

        def part2():
            for half in range(2):
                bk = alloc_bank()
                for gl in range(4):
                    g = 4 * half + gl
                    mm(ps[:, bk, gl * 128:(gl + 1) * 128], ones_1[:, :], WsT[:, g, :], True, True,
                       r=[bONES, bR3], w=[PSB[bk]])
                for gl in range(4):
                    g = 4 * half + gl
                    STT(Cb[:, g * 128:(g + 1) * 128], ps[:, bk, gl * 128:(gl + 1) * 128],
                        cvcol("gmlp_ln_b", j * 8 + g), Cb[:, g * 128:(g + 1) * 128], ALU.mult, ALU.add,
                        r=[PSB[bk], bCOLV, bR1], w=[bR1])
        return [part0, part1, part2]

    def ln_stats(bk, npart, vq):
        o = vq * 16
        bsm = SMALLB[vq]
        P.op("vector", lambda e: e.bn_stats(out=small[0:npart, o:o + 6], in_=ps[0:npart, bk, :]),
             r=[PSB[bk]], w=[bsm])
        P.op("vector", lambda e: e.bn_stats(out=small[0:npart, o + 6:o + 12], in_=ps[0:npart, bk + 1, :]),
             r=[PSB[bk + 1]], w=[bsm])
        P.op("vector", lambda e: e.bn_aggr(out=small[0:npart, o + 12:o + 14], in_=small[0:npart, o:o + 12]),
             r=[bsm], w=[bsm])
        rsqrt_act(small[0:npart, o + 13:o + 14], small[0:npart, o + 13:o + 14], npart, r=[bsm], w=[bsm])
        TS(small[0:npart, o + 14:o + 15], small[0:npart, o + 12:o + 13], small[0:npart, o + 13:o + 14], -1.0,
           ALU.mult, ALU.mult, r=[bsm], w=[bsm])
        return small[0:npart, o + 13:o + 14], small[0:npart, o + 14:o + 15]

    def make_gmlp(i):
        par = i % 2
        j = i // 2
        Cb = R1
        WsT = R3.bitcast(BF16).rearrange("p (g t) -> p g t", g=8)
        vt = [R2[:, 0:512].bitcast(BF16), R2[:, 512:1024].bitcast(BF16), arena[:, 0:512].bitcast(BF16)]
        bvt = [bR2a, bR2b, bR0a]
        gs = {"vs": None, "pend": [], "cnt": 2, "samp": False}

        def vA(tt):
            if gs["vs"] is None:
                gs["vs"] = [wget_cols(gmlp_w_in[j], 1024 + t * 256) for t in range(4)]
            samp = tt == NTT
            npart = NS if samp else 128
            hc0 = 16 + tt * 128
            b = SB if samp else tt // 4
            bk = alloc_bank(2)
            for t in range(4):
                s, wv = gs["vs"][t]
                for k in range(8):
                    mm(ps[0:npart, bk + t // 2, (t % 2) * 256:(t % 2 + 1) * 256], hT[:, k, hc0:hc0 + npart],
                       wv[:, k, :], k == 0, k == 7, r=[SLOT[s], H[k][b]], w=[PSB[bk + t // 2]])
            if not samp:
                vq = gs["cnt"] % 3
                gs["cnt"] += 1
                rs_ap, nb_ap = ln_stats(bk, npart, vq)
                ACT(vt[vq].rearrange("p (a n) -> p a n", a=2), ps[:, bk:bk + 2, :], AF.Identity,
                    r=[PSB[bk], PSB[bk + 1], SMALLB[vq]], w=[bvt[vq]], bias=nb_ap, scale=rs_ap)
            else:
                vq = 0
                rs_ap, nb_ap = ln_stats(bk, npart, vq)
                zs = R2[0:NS, :]
                ACT(zs.rearrange("p (a n) -> p a n", a=2), ps[0:NS, bk:bk + 2, :], AF.Identity,
                    r=[PSB[bk], PSB[bk + 1], SMALLB[vq]], w=bR2, bias=nb_ap, scale=rs_ap)
            return vq

        def vB(tt, vq):
            samp = tt == NTT
            if not samp:
                b = tt // 4
                for half in range(2):
                    bs_ = alloc_bank()
                    for gl in range(4):
                        g = 4 * half + gl
                        mm(ps[:, bs_, gl * 128:(gl + 1) * 128], vt[vq][:, g * 128:(g + 1) * 128],
                           WsT[:, g, :], True, True, r=[bvt[vq], bR3], w=[PSB[bs_]])
                    for gl in range(4):
                        g = 4 * half + gl
                        STT(wk[:, g, tt * 128:(tt + 1) * 128], ps[:, bs_, gl * 128:(gl + 1) * 128],
                            cvcol("gmlp_ln_g", j * 8 + g), Cb[:, g * 128:(g + 1) * 128], ALU.mult, ALU.add,
                            r=[PSB[bs_], bCOLV, bR1], w=[WK[g][b]])
            else:
                bz = alloc_bank()
                for g in range(8):
                    tr(ps[:, bz, g * 16:(g + 1) * 16], R2[0:NS, g * 128:(g + 1) * 128], ident[0:NS, 0:NS],
                       r=bR2 + [bCST], w=[PSB[bz]])
                for g in range(8):
                    TS(vnT[:, g, :], ps[:, bz, g * 16:(g + 1) * 16], cvcol("gmlp_ln_g", j * 8 + g),
                       cvcol("gmlp_ln_b", j * 8 + g), ALU.mult, ALU.add, r=[PSB[bz], bCOLV], w=[bVNT])
                tok_major_out(vnT, bVNT, lambda stage: (ncv[j, :, :], stage[0:16, :]), R2, bR2, "out_v")
                for g in range(8):
                    TS(wk[:, g, SEQ:SEQ + NS], vnT[:, g, :], ws00[:, g:g + 1], bs0[:, g:g + 1], ALU.mult, ALU.add,
                       r=[bVNT, bWS00, bBS0], w=[WK[g][SB]])

        def v_stage(b):
            if b == SB:
                vA(NTT)
                gs["samp"] = True
                return
            for tt in range(4 * b, 4 * b + 4):
                vq = vA(tt)
                if gs["samp"]:
                    vB(NTT, 0)
                    gs["samp"] = False
                gs["pend"].append((tt, vq))
                if len(gs["pend"]) > 2:
                    vB(*gs["pend"].pop(0))

        def v_flush():
            assert not gs["samp"]
            while gs["pend"]:
                vB(*gs["pend"].pop(0))
            for s, _ in gs["vs"]:
                wrel(s)

        def u_phase():
            for t in range(4):
                s, wv = wget_cols(gmlp_w_in[j], t * 256)
                for oc2 in range(2):
                    oc = 2 * t + oc2
                    for b in ORDER:
                        col0, n = blocks[b]
                        bk = alloc_bank()
                        for k in range(8):
                            mm(ps[:, bk, 0:n], wv[:, k, oc2 * 128:(oc2 + 1) * 128],
                               hT[:, k, 16 + col0:16 + col0 + n], k == 0, k == 7,
                               r=[SLOT[s], H[k][b]], w=[PSB[bk]])
                        TT(wk[:, oc, col0:col0 + n], ps[:, bk, 0:n], wk[:, oc, col0:col0 + n], ALU.mult,
                           r=[PSB[bk], WK[oc][b]], w=[WK[oc][b]])
                wrel(s)
                if ada_allow(i + 1):
                    ada_pop(1)

        outp = make_outproj(i, gmlp_w_out[j], lambda k, col0, n: wk[:, k, col0:col0 + n], WK, 2)
        return v_stage, v_flush, u_phase, outp

    def make_ffn(i):
        par = i % 2
        groups = [[0, 1, 2, 3], [4, 5, 6, 7], [8, 9, 10]]

        def gate_up(grp):
            for ti, t in enumerate(grp):
                sg, gvw = wget_cols(ffn_w_gate[i], t * 256)
                su, uvw = wget_cols(ffn_w_up[i], t * 256)
                for fc in range(2):
                    wc = 2 * ti + fc
                    for b in ORDER:
                        col0, n = blocks[b]
                        bg = alloc_bank()
                        for k in range(8):
                            mm(ps[:, bg, 0:n], gvw[:, k, fc * 128:(fc + 1) * 128], hT[:, k, 16 + col0:16 + col0 + n],
                               k == 0, k == 7, r=[SLOT[sg], H[k][b]], w=[PSB[bg]])
                        bu = alloc_bank()
                        for k in range(8):
                            mm(ps[:, bu, 0:n], uvw[:, k, fc * 128:(fc + 1) * 128], hT[:, k, 16 + col0:16 + col0 + n],
                               k == 0, k == 7, r=[SLOT[su], H[k][b]], w=[PSB[bu]])
                        q = nxt("sq", 3)
                        ACT(sq[:, q, 0:n], ps[:, bg, 0:n], AF.Silu, r=[PSB[bg]], w=[SQ[q]])
                        TT(wk[:, wc, col0:col0 + n], ps[:, bu, 0:n], sq[:, q, 0:n], ALU.mult,
                           r=[PSB[bu], SQ[q]], w=[WK[wc][b]])
                wrel(sg)
                wrel(su)
                if ada_allow(i + 1):
                    ada_pop(2)

        def make_down(grp):
            st_ = {"sd": None, "done": 0}
            nk = 2 * len(grp)

            def stage(b, co=None):
                if st_["sd"] is None:
                    ada_need(i, 5)
                    st_["sd"] = [wget_rows(ffn_w_down[i], t * 256) for t in grp]
                col0, n = blocks[b]
                if co is not None:
                    co(-1)
                for oc in range(8):
                    bk = alloc_bank()
                    for kk in range(nk):
                        s, dv = st_["sd"][kk // 2]
                        mm(ps[:, bk, 0:n], dv[:, kk % 2, oc * 128:(oc + 1) * 128], wk[:, kk, col0:col0 + n],
                           kk == 0, kk == nk - 1, r=[SLOT[s], WK[kk][b]], w=[PSB[bk]])
                    if co is not None:
                        co(oc)
                    x_update(bk, oc, b, par, 5)
                st_["done"] += 1
                if st_["done"] == NB:
                    for s, _ in st_["sd"]:
                        wrel(s)
            return stage

        def front(hooks=()):
            hooks = list(hooks)
            if hooks:
                hooks.pop(0)()
            for gi, grp in enumerate(groups):
                gate_up(grp)
                if hooks:
                    hooks.pop(0)()
                if gi < len(groups) - 1:
                    dn = make_down(grp)
                    for b in ORDER:
                        dn(b)
                    if ada_allow(i + 1):
                        ada_pop(1)
            while hooks:
                hooks.pop(0)()
            return make_down(groups[-1])
        return front

    fin = {"ot": 0}
    ostg = [R0, R1]
    bostg = [bR0, [bR1]]

    gfb = tmp[:, 0:2, :].rearrange("p a n -> p (a n)")

    def final_prep():
        DMA("sync", gfb, norm_final.partition_broadcast(128), "gfb", w=[TMP[0], TMP[1]])

    def final_b(b):
        col0, n = blocks[b]
        nsub = 1 if b == SB else n // 128
        npart = NS if b == SB else 128
        for s_ in range(nsub):
            c0 = col0 + s_ * 128
            bb = alloc_bank(2)
            for c in range(8):
                tr(ps[0:npart, bb + c // 4, (c % 4) * 128:(c % 4 + 1) * 128], xT[:, c, c0:c0 + npart], ident,
                   r=[X[c][b], bCST], w=[PSB[bb + c // 4]])
            vq = fin["ot"] % 3
            o = vq * 16
            bsm = SMALLB[vq]
            P.op("vector", lambda e, o=o, bb=bb: e.bn_stats(out=small[0:npart, o:o + 6], in_=ps[0:npart, bb, :]),
                 r=[PSB[bb]], w=[bsm])
            P.op("vector", lambda e, o=o, bb=bb: e.bn_stats(out=small[0:npart, o + 6:o + 12],
                                                             in_=ps[0:npart, bb + 1, :]),
                 r=[PSB[bb + 1]], w=[bsm])
            P.op("vector", lambda e, o=o: e.bn_aggr(out=small[0:npart, o + 12:o + 14], in_=small[0:npart, o:o + 12]),
                 r=[bsm], w=[bsm])
            STT(small[0:npart, o + 14:o + 15], small[0:npart, o + 12:o + 13], small[0:npart, o + 12:o + 13],
                small[0:npart, o + 13:o + 14], ALU.mult, ALU.add, r=[bsm], w=[bsm])
            rsqrt_act(small[0:npart, o + 14:o + 15], small[0:npart, o + 14:o + 15], npart, r=[bsm], w=[bsm])
            q = fin["ot"] % 2
            fin["ot"] += 1
            ACT(ostg[q][0:npart, :].rearrange("p (a n) -> p a n", a=2), ps[0:npart, bb:bb + 2, :], AF.Identity,
                r=[PSB[bb], PSB[bb + 1], bsm], w=bostg[q], scale=small[0:npart, o + 14:o + 15])
            P.op("gpsimd", lambda e, q=q: e.tensor_tensor(out=ostg[q][0:npart, :], in0=ostg[q][0:npart, :],
                                                            in1=gfb[0:npart, :], op=ALU.mult),
                 r=bostg[q] + [TMP[0], TMP[1]], w=bostg[q])
            if b == SB:
                DMA("sync", ys[:, :], ostg[q][0:NS, :], "out_y%d" % q, r=bostg[q])
            else:
                DMA("sync", yp[c0:c0 + 128, :], ostg[q], "out_y%d" % q, r=bostg[q])

    ada_need(0, 1)
    DMA("gpsimd", pmat[:, :], pmat_d[:, :], "pm", w=[bPMAT])
    prev_stage = xload_block
    for i in range(DEPTH):
        pool_layer = (i % 2 == 0)
        n1a = (lambda b, c, i=i: norm_a((i, 0), b, c))
        n1b = (lambda b, c, i=i, pl=pool_layer: norm_b(i, 0, pl, b, c))
        n2a = (lambda b, c, i=i: norm_a((i, 1), b, c))
        n2b = (lambda b, c, i=i, pl=pool_layer: norm_b(i, 1, pl, b, c))

        stepc = {"n": 0}

        def step_pop(i=i, stepc=stepc):
            if ada_allow(i):
                if i == 0:
                    ada_pop(1 if stepc["n"] < 5 else 3)
                else:
                    ada_pop(2)
            stepc["n"] += 1

        if pool_layer:
            zpool, outp = make_pool(i)
            pipeline([(prev_stage, 0, [(n1a, 1), (n1b, 2)]), (zpool, 3), (outp, 4, [(n2a, 5), (n2b, 6)])],
                     ORDER, per_step=step_pop)
        else:
            v_stage, v_flush, u_phase, outp = make_gmlp(i)
            pipeline([(prev_stage, 0, [(n1a, 1), (n1b, 2)]), (v_stage, 3)], ORDER, per_step=step_pop)
            v_flush()
            u_phase()
            pipeline([(outp, 0, [(n2a, 1), (n2b, 2)])], ORDER, per_step=step_pop)
        ada_need(i, 5)
        if i == 0:
            for j in range(NPOOL):
                DMA("sync", nps[j, :, 0:PB - 1, :], sp[j, :, 1:PB, :], "out_hist")
        nxt_prep = gmlp_prep_parts(i + 1) if (i + 1 < DEPTH and (i + 1) % 2 == 1) else []
        prev_stage = make_ffn(i)(nxt_prep)
    final_prep()
    pipeline([(prev_stage, 0), (final_b, 1)], ORDER)

    finals = ["out_hist", "out_np", "out_ns", "out_v", "out_y0", "out_y1"]
    stats = P.emit(nc, st, finals)
    st.close()
    return nc, stats


def make_cst():
    c = np.zeros((128, NCST), np.float32)
    c[:, 0:128] = np.eye(128, dtype=np.float32)
    s_ = np.arange(128)[:, None]
    t_ = np.arange(128)[None, :]
    c[:, 128:256] = (s_ <= t_).astype(np.float32)
    for wi, w in enumerate(WINDOWS):
        for t in range(16):
            c[:, 256 + wi * 16 + t] = 1.0 / min(t + 1, w)
        for r in range(120):
            bl, tt = divmod(r, PB)
            if tt >= 16 - w:
                c[r, 320 + wi * 8 + bl] = 1.0
    c[0, 352:480] = 1.0
    return c


def make_pmat():
    m = np.zeros((128, 12 * 128), np.float32)
    s_ = np.arange(128)[:, None]
    t_ = np.arange(128)[None, :]
    for wi, w in enumerate(WINDOWS):
        pa = ((s_ <= t_) & (s_ > t_ - w)).astype(np.float32) / w - (s_ == t_).astype(np.float32)
        pb = (s_ >= 129 + t_ - w).astype(np.float32) / w
        n_t = np.minimum(t_ + 1, w).astype(np.float32)
        pa0 = ((s_ <= t_) & (s_ > t_ - w)).astype(np.float32) / n_t - (s_ == t_).astype(np.float32)
        m[:, (0 + wi) * 128:(0 + wi + 1) * 128] = pa
        m[:, (4 + wi) * 128:(4 + wi + 1) * 128] = pb
        m[:, (8 + wi) * 128:(8 + wi + 1) * 128] = pa0
    return m


_WNAMES = ("w_ada", "b_ada", "norm_mix", "norm_ffn", "norm_final", "pool_w_grp", "pool_scale", "pool_w_out",
           "gmlp_w_in", "gmlp_ln_g", "gmlp_ln_b", "gmlp_w_s", "gmlp_b_s", "gmlp_w_out",
           "ffn_w_gate", "ffn_w_up", "ffn_w_down")

_CACHE = {}


def run(inputs, cfg, ncores):
    key = (cfg.SEQ, cfg.DEPTH)
    if key not in _CACHE:
        _CACHE[key] = build_program(cfg)
    nc, _ = _CACHE[key]
    f = lambda a: np.ascontiguousarray(np.asarray(a, dtype=np.float32))
    wts = {n: f(inputs[n]) for n in _WNAMES}
    cstv = make_cst()
    pmatv = make_pmat()
    xpr, xsm, spl = f(inputs["x_prompt"]), f(inputs["x_sample"]), f(inputs["state_pool"])
    cp, cs = f(inputs["c_prompt"]), f(inputs["c_sample"])
    in_maps = []
    for r in range(ncores):
        m = dict(wts)
        m["xp"] = xpr[r]
        m["xs"] = np.ascontiguousarray(xsm[NS * r:NS * (r + 1), 0, :])
        m["sp"] = np.ascontiguousarray(spl[:, NS * r:NS * (r + 1)])
        m["cc"] = np.ascontiguousarray(np.concatenate([cp[r:r + 1], cs[NS * r:NS * (r + 1)]], axis=0))
        m["cst"] = cstv
        m["pmat"] = pmatv
        in_maps.append(m)
    res = run_bass_kernel_spmd(nc, in_maps, core_ids=list(range(ncores)))
    rs = res.results
    y_prompt = np.stack([rs[r]["yp"] for r in range(ncores)], axis=0)
    y_sample = np.concatenate([rs[r]["ys"] for r in range(ncores)], axis=0)[:, None, :]
    npp = np.stack([rs[r]["npp"] for r in range(ncores)], axis=1)
    nps = np.concatenate([rs[r]["nps"] for r in range(ncores)], axis=1)
    ncv = np.concatenate([rs[r]["ncv"] for r in range(ncores)], axis=1)[:, :, None, :]
    return (y_prompt.astype(np.float32), y_sample.astype(np.float32), npp.astype(np.float32),
            nps.astype(np.float32), ncv.astype(np.float32))


def kernel(**inputs):
    return run(inputs, Cfg(2048, 4), NCORES)
```

```python
import numpy as np
from contextlib import ExitStack

import concourse.bass as bass
import concourse.mybir as mybir
from concourse.bass_utils import run_bass_kernel_spmd

F32 = mybir.dt.float32
BF16 = mybir.dt.bfloat16
AF = mybir.ActivationFunctionType
ALU = mybir.AluOpType

D = 1024
NCH = 8
FF = 2816
NFT = 11
WINDOWS = (2, 4, 8, 16)
PB = 15
EPS = 1e-6
NS = 16
NCORES = 8
NSLOT = 8
NCST = 480

ENGS = ("tensor", "scalar", "vector", "gpsimd", "sync")


class Buf:
    __slots__ = ("key", "w", "r")

    def __init__(self, key):
        self.key = key
        self.w = {}
        self.r = {}


class Op:
    __slots__ = ("fn", "deps", "dma")

    def __init__(self, fn, deps, dma):
        self.fn = fn
        self.deps = deps
        self.dma = dma


class Plan:
    def __init__(self):
        self.ops = {e: [] for e in ENGS}
        self.dmacnt = {}
        self.bufs = {}
        self.final_sems = set()

    def buf(self, *key):
        b = self.bufs.get(key)
        if b is None:
            b = self.bufs[key] = Buf(key)
        return b

    def op(self, eng, fn, r=(), w=(), dma=None):
        idx = len(self.ops[eng])
        deps = {}

        def add(tok):
            k = (tok[0], tok[1])
            old = deps.get(k)
            if old is None or (tok[2] is not None and old[2] is not None and tok[2] > old[2]):
                deps[k] = tok

        is_dma = dma is not None
        for b in r:
            for tok in b.w.values():
                add(tok)
        for b in w:
            for tok in b.w.values():
                if is_dma and tok[0] == "d" and tok[1] == dma:
                    continue
                if is_dma or tok[0] == "d" or tok[1] != eng or eng != "tensor":
                    add(tok)
            for tok in b.r.values():
                if is_dma or tok[0] == "d" or tok[1] != eng or eng != "tensor":
                    add(tok)
        if is_dma:
            self.dmacnt[dma] = self.dmacnt.get(dma, 0) + 16
            tok = ("d", dma, None if dma in self.final_sems else self.dmacnt[dma])
        else:
            tok = ("e", eng, idx)
        for b in r:
            b.r[(tok[0], tok[1])] = tok
        for b in w:
            b.w[(tok[0], tok[1])] = tok
            b.r = {}
        self.ops[eng].append(Op(fn, list(deps.values()), dma))
        return tok

    def emit(self, nc, stack, final_waits):
        sig = {e: [False] * len(self.ops[e]) for e in ENGS}
        for e in ENGS:
            for o in self.ops[e]:
                for t in o.deps:
                    if t[0] == "e":
                        sig[t[1]][t[2]] = True
        cum = {}
        for e in ENGS:
            c = 0
            arr = []
            for i in range(len(self.ops[e])):
                if sig[e][i]:
                    c += 1
                arr.append(c)
            cum[e] = arr
        names = set()
        for e in ENGS:
            for i, o in enumerate(self.ops[e]):
                if o.dma is not None:
                    names.add("D_" + o.dma)
                elif sig[e][i]:
                    names.add("E_" + e)
        sems = {n: stack.enter_context(nc.semaphore(n)) for n in sorted(names)}
        stats = {}
        with nc.Block() as block:
            for e in ENGS:
                def body(engine, e=e):
                    waited = {}
                    nwait = 0
                    for i, o in enumerate(self.ops[e]):
                        for t in o.deps:
                            if t[0] == "e":
                                name = "E_" + t[1]
                                val = cum[t[1]][t[2]]
                            else:
                                name = "D_" + t[1]
                                val = t[2] if t[2] is not None else self.dmacnt[t[1]]
                            if waited.get(name, 0) < val:
                                engine.wait_ge(sems[name], val)
                                waited[name] = val
                                nwait += 1
                        ins = o.fn(engine)
                        if o.dma is not None:
                            ins.then_inc(sems["D_" + o.dma], 16)
                        elif sig[e][i]:
                            ins.then_inc(sems["E_" + e], 1)
                    if e == "sync":
                        for sname in final_waits:
                            if sname in self.dmacnt:
                                engine.wait_ge(sems["D_" + sname], self.dmacnt[sname])
                    stats[e] = (len(self.ops[e]), nwait)
                getattr(block, e)(body)
        return stats


class Cfg:
    def __init__(self, SEQ=2048, DEPTH=4):
        self.SEQ = SEQ
        self.DEPTH = DEPTH


def cv_layout(DEPTH):
    NPOOL = (DEPTH + 1) // 2
    NG = max(DEPTH // 2, 1)
    segs = [("b_ada", DEPTH * 48), ("norm_mix", DEPTH * 8), ("norm_ffn", DEPTH * 8),
            ("norm_final", 8), ("pool_scale", NPOOL * 8), ("gmlp_ln_g", NG * 8), ("gmlp_ln_b", NG * 8)]
    off = {}
    o = 0
    for n, k in segs:
        off[n] = o
        o += k
    return segs, off, o


def build_program(cfg):
    SEQ, DEPTH = cfg.SEQ, cfg.DEPTH
    NPB = SEQ // 512
    NTT = SEQ // 128
    TOK = SEQ + NS
    blocks = [(512 * b, 512) for b in range(NPB)] + [(SEQ, NS)]
    NB = len(blocks)
    SB = NB - 1
    NPOOL = (DEPTH + 1) // 2
    NG = DEPTH // 2
    NGA = max(NG, 1)

    nc = bass.Bass("TRN2", target_bir_lowering=False)

    def din(name, shape):
        return nc.dram_tensor(name, list(shape), F32, kind="ExternalInput").ap()

    def dout(name, shape):
        return nc.dram_tensor(name, list(shape), F32, kind="ExternalOutput").ap()

    xp = din("xp", [SEQ, D])
    xs = din("xs", [NS, D])
    sp = din("sp", [NPOOL, NS, PB, D])
    cc = din("cc", [1 + NS, D])
    cst = din("cst", [128, NCST])
    pmat_d = din("pmat", [128, 12 * 128])
    w_ada = din("w_ada", [DEPTH, D, 6 * D])
    b_ada = din("b_ada", [DEPTH, 6 * D])
    norm_mix = din("norm_mix", [DEPTH, D])
    norm_ffn = din("norm_ffn", [DEPTH, D])
    norm_final = din("norm_final", [D])
    pool_w_grp = din("pool_w_grp", [NPOOL, 4, 256, 256])
    pool_scale = din("pool_scale", [NPOOL, D])
    pool_w_out = din("pool_w_out", [NPOOL, D, D])
    gmlp_w_in = din("gmlp_w_in", [NGA, D, 2 * D])
    gmlp_ln_g = din("gmlp_ln_g", [NGA, D])
    gmlp_ln_b = din("gmlp_ln_b", [NGA, D])
    gmlp_w_s = din("gmlp_w_s", [NGA, 8, 128, 128])
    gmlp_b_s = din("gmlp_b_s", [NGA, 8, 128])
    gmlp_w_out = din("gmlp_w_out", [NGA, D, D])
    ffn_w_gate = din("ffn_w_gate", [DEPTH, D, FF])
    ffn_w_up = din("ffn_w_up", [DEPTH, D, FF])
    ffn_w_down = din("ffn_w_down", [DEPTH, FF, D])

    yp = dout("yp", [SEQ, D])
    ys = dout("ys", [NS, D])
    npp = dout("npp", [NPOOL, PB, D])
    nps = dout("nps", [NPOOL, NS, PB, D])
    ncv = dout("ncv", [NGA, NS, D])

    segs, cvoff, NCV = cv_layout(DEPTH)
    NCVT = (NCV + 127) // 128

    P = Plan()
    P.final_sems.add("init")
    st = ExitStack()

    def sb(name, shape, dt):
        return st.enter_context(nc.sbuf_tensor(name, list(shape), dt))

    xT = sb("xT", [128, 8, TOK], F32)
    hT = sb("hT", [128, 8, 16 + TOK], BF16)
    wk = sb("wk", [128, 8, TOK], BF16)
    wsl = sb("wsl", [128, NSLOT, 2048], BF16)
    mod = sb("mod", [128, 2, 48, 17], F32)
    Amod = sb("Amod", [128, 2, 2, 8, 17], F32)
    colv = sb("colv", [128, NCVT * 128], F32)
    cstt = sb("cstt", [128, NCST], F32)
    ones_m = sb("ones_m", [128, 128], BF16)
    ones_1 = sb("ones_1", [128, 128], BF16)
    scT = sb("scT", [128, 8, 17], BF16)
    rstd = sb("rstd", [128, 2, 512], F32)
    tmp = sb("tmp", [128, 3, 512], F32)
    sq = sb("sq", [128, 3, 512], BF16)
    hs32 = sb("hs32", [128, 8, 16], F32)
    hlast = sb("hlast", [128, 8, 16], F32)
    t16 = sb("t16", [128, 8, 16], F32)
    t16b = sb("t16b", [128, 2, 16], F32)
    small = sb("small", [128, 64], F32)
    ws00 = sb("ws00", [128, 8], F32)
    bs0 = sb("bs0", [128, 8], F32)
    vnT = sb("vnT", [128, 8, 16], F32)
    arena = sb("arena", [128, 3680], F32)
    epsc = sb("epsc", [128, 1], F32)
    pmat = sb("pmat_sb", [128, 12 * 128], BF16)
    pl16 = sb("pl16", [128, 8, 16], BF16)
    fz = sb("fz", [128, 1], F32)
    ps = st.enter_context(nc.psum_tensor("ps", [128, 8, 512], F32))

    R0a = arena[:, 0:512]
    R0b = arena[:, 512:1056]
    R0 = arena[:, 0:1024]
    R1 = arena[:, 1056:2080]
    R2 = arena[:, 2112:3136]
    R3 = arena[:, 3168:3680]
    bR0a, bR0b, bR1 = P.buf("R0a"), P.buf("R0b"), P.buf("R1")
    bR2a, bR2b, bR3 = P.buf("R2a"), P.buf("R2b"), P.buf("R3")
    bR0 = [bR0a, bR0b]
    bR2 = [bR2a, bR2b]

    ident = cstt[:, 0:128]
    maskT = cstt[:, 128:256]
    rc = cstt[:, 256:320]
    sel = cstt[:, 320:352]
    E0 = cstt[:, 352:480]
    bCST = P.buf("cst")
    bONES = P.buf("ones")
    bCOLV = P.buf("colv")
    bSCT = P.buf("scT")
    bPMAT = P.buf("pmat")
    bPL16 = P.buf("pl16")

    X = [[P.buf("x", c, b) for b in range(NB)] for c in range(8)]
    H = [[P.buf("h", c, b) for b in range(NB)] for c in range(8)]
    WK = [[P.buf("wk", c, b) for b in range(NB)] for c in range(8)]
    PSB = [P.buf("ps", k) for k in range(8)]
    SLOT = [P.buf("slot", s) for s in range(NSLOT)]
    MODB = [[P.buf("mod", p, m) for m in range(6)] for p in range(2)]
    SMALLB = [P.buf("small", q) for q in range(3)]
    AMOD = [[P.buf("amod", p, wh) for wh in range(2)] for p in range(2)]
    RSTD = [P.buf("rstd", q) for q in range(2)]
    TMP = [P.buf("tmp", q) for q in range(3)]
    SQ = [P.buf("sq", q) for q in range(3)]
    bHS32, bHLAST, bT16, bT16B = P.buf("hs32"), P.buf("hlast"), P.buf("t16"), P.buf("t16b")
    bSMALL, bWS00, bBS0, bVNT = P.buf("small"), P.buf("ws00"), P.buf("bs0"), P.buf("vnT")

    state = {"bank": 0, "rq": 0, "tq": 0, "sq": 0, "t16b": 0}
    free_slots = list(range(NSLOT))

    held_banks = set()

    def alloc_bank(n=1):
        p = state["bank"]
        for _ in range(32):
            if n == 2 and p % 2 == 1:
                p = (p + 1) % 8
            if p + n > 8:
                p = 0
            if all((p + k_) not in held_banks for k_ in range(n)):
                state["bank"] = (p + n) % 8
                return p
            p = (p + 1) % 8
        raise RuntimeError("no free PSUM bank")

    def nxt(key, n):
        v = state[key]
        state[key] = (v + 1) % n
        return v

    def slot_alloc():
        assert free_slots, "weight ring exhausted"
        return free_slots.pop(0)

    def wrel(s):
        assert s not in free_slots
        free_slots.append(s)

    def wget_cols(W2d, col0, ncols=256):
        s = slot_alloc()
        src = W2d[:, col0:col0 + ncols].rearrange("(k p) n -> p k n", p=128)
        dst = wsl[:, s, 0:8 * ncols].rearrange("p (k n) -> p k n", k=8)
        P.op("gpsimd", lambda e: e.dma_start(out=dst, in_=src), w=[SLOT[s]], dma="w%d" % s)
        return s, dst

    def wget_rows(W2d, row0):
        s = slot_alloc()
        src = W2d[row0:row0 + 256, :].rearrange("(j p) n -> p j n", p=128)
        dst = wsl[:, s, :].rearrange("p (j n) -> p j n", j=2)
        P.op("gpsimd", lambda e: e.dma_start(out=dst, in_=src), w=[SLOT[s]], dma="w%d" % s)
        return s, dst

    def cvcol(name, row):
        o = cvoff[name] + row
        return colv[:, o:o + 1]

    def mm(out, lhsT, rhs, start, stop, r, w):
        P.op("tensor", lambda e: e.matmul(out, lhsT, rhs, start=start, stop=stop), r=r, w=w)

    def tr(out, in_, idn, r, w):
        P.op("tensor", lambda e: e.transpose(out, in_, idn), r=r, w=w)

    def ACT(out, in_, func, r, w, bias=None, scale=None):
        kw = {}
        if bias is not None:
            kw["bias"] = bias
        if scale is not None:
            kw["scale"] = scale
        P.op("scalar", lambda e: e.activation(out=out, in_=in_, func=func, **kw), r=r, w=w)

    def ACOPY(out, in_, r, w):
        P.op("scalar", lambda e: e.copy(out=out, in_=in_), r=r, w=w)

    def VCOPY(out, in_, r, w):
        P.op("vector", lambda e: e.tensor_copy(out=out, in_=in_), r=r, w=w)

    def TT(out, in0, in1, op, r, w):
        P.op("vector", lambda e: e.tensor_tensor(out=out, in0=in0, in1=in1, op=op), r=r, w=w)

    def TS(out, in0, s1, s2, op0, op1, r, w):
        if op1 is None:
            P.op("vector", lambda e: e.tensor_scalar(out=out, in0=in0, scalar1=s1, scalar2=None, op0=op0), r=r, w=w)
        else:
            P.op("vector", lambda e: e.tensor_scalar(out=out, in0=in0, scalar1=s1, scalar2=s2, op0=op0, op1=op1),
                 r=r, w=w)

    def STT(out, in0, scalar, in1, op0, op1, r, w):
        P.op("vector", lambda e: e.scalar_tensor_tensor(out=out, in0=in0, scalar=scalar, in1=in1, op0=op0, op1=op1),
             r=r, w=w)

    def DMA(eng, out, in_, sem, r=(), w=()):
        P.op(eng, lambda e: e.dma_start(out=out, in_=in_), r=r, w=w, dma=sem)

    def MEMSET(ap, val, w):
        P.op("vector", lambda e: e.memset(ap, val), w=w)

    ZTH = [[P.buf("zt", zi, hf) for hf in range(2)] for zi in range(3)]
    ALIAS_R23 = [bR2a, bR2b, bR3] + [b_ for pr in ZTH for b_ in pr]

    def FENCE(bufs):
        P.op("vector", lambda e: e.memset(fz[:, :], 0.0), w=bufs)

    def rsqrt_act(out, in_, npart, r, w):
        ACT(out, in_, AF.Ln, r=list(r) + [bONES], w=w, bias=epsc[0:npart, :], scale=1.0)
        ACT(out, out, AF.Exp, r=w, w=w, scale=-0.5)

    def pipeline(items, order, per_step=None):
        n = len(order)
        max_off = 0
        for it in items:
            max_off = max(max_off, it[1])
            if len(it) > 2:
                for _, co_off in it[2]:
                    max_off = max(max_off, co_off)
        for step in range(n + max_off):
            if per_step is not None:
                per_step()
            for it in items:
                fn, off = it[0], it[1]
                jj = step - off
                has_fn = 0 <= jj < n
                if len(it) > 2:
                    act = [(cf, order[step - co]) for cf, co in it[2] if 0 <= step - co < n]

                    def co(c, act=act):
                        for cf, bb in act:
                            cf(bb, c)
                    if has_fn:
                        fn(order[jj], co if act else None)
                    elif act:
                        for c in range(-1, 8):
                            co(c)
                elif has_fn:
                    fn(order[jj])

    ORDER = [SB] + list(range(NPB))

    DMA("sync", cstt[:, :], cst[:, :], "init", w=[bCST])
    MEMSET(ones_m[:, :], 1.0 / 1024.0, [bONES])
    MEMSET(ones_1[:, :], 1.0, [bONES])
    MEMSET(epsc[:, :], EPS, [bONES])
    MEMSET(hT[:, :, 0:16], 0.0, [H[c][0] for c in range(8)])


    cvst = R3
    srcs = {
        "b_ada": b_ada.rearrange("l (r p) -> (l r) p", p=128),
        "norm_mix": norm_mix.rearrange("l (r p) -> (l r) p", p=128),
        "norm_ffn": norm_ffn.rearrange("l (r p) -> (l r) p", p=128),
        "norm_final": norm_final.rearrange("(r p) -> r p", p=128),
        "pool_scale": pool_scale.rearrange("l (r p) -> (l r) p", p=128),
        "gmlp_ln_g": gmlp_ln_g.rearrange("l (r p) -> (l r) p", p=128),
        "gmlp_ln_b": gmlp_ln_b.rearrange("l (r p) -> (l r) p", p=128),
    }
    assert NCVT <= 4
    for name, nrows in segs:
        o = cvoff[name]
        done = 0
        while done < nrows:
            t = (o + done) // 128
            p0 = (o + done) % 128
            n = min(nrows - done, 128 - p0)
            DMA("sync", cvst[p0:p0 + n, t * 128:(t + 1) * 128], srcs[name][done:done + n, :], "init", w=[bR3])
            done += n
    for t in range(NCVT):
        nrow = min(128, NCV - t * 128)
        bk = alloc_bank()
        tr(ps[:, bk, 0:nrow], cvst[0:nrow, t * 128:(t + 1) * 128], ident[0:nrow, 0:nrow],
           r=[bR3, bCST], w=[PSB[bk]])
        VCOPY(colv[:, t * 128:t * 128 + nrow], ps[:, bk, 0:nrow], r=[PSB[bk]], w=[bCOLV])

    DMA("sync", R2[0:1 + NS, :], cc[:, :], "init", w=bR2)
    bk = alloc_bank()
    for c in range(8):
        tr(ps[:, bk, c * 17:(c + 1) * 17], R2[0:17, c * 128:(c + 1) * 128], ident[0:17, 0:17],
           r=bR2 + [bCST], w=[PSB[bk]])
    ACT(scT[:, :, :], ps[:, bk, 0:136].rearrange("p (c n) -> p c n", c=8), AF.Silu, r=[PSB[bk]], w=[bSCT])

    ada_state = {"layer": 0, "next": 0}

    def ada_tile(i, t):
        par = i % 2
        m = t // 4
        s, wv = wget_cols(w_ada[i], t * 256)
        for oc in range(2):
            ch = 2 * t + oc
            bk = alloc_bank()
            for k in range(8):
                mm(ps[:, bk, 0:17], wv[:, k, oc * 128:(oc + 1) * 128], scT[:, k, :], k == 0, k == 7,
                   r=[SLOT[s], bSCT], w=[PSB[bk]])
            TS(mod[:, par, ch, :], ps[:, bk, 0:17], cvcol("b_ada", i * 48 + ch), None, ALU.add, None,
               r=[PSB[bk], bCOLV], w=[MODB[par][m]])
        wrel(s)
        if t % 4 == 3 and m in (1, 4):
            wh = 0 if m == 1 else 1
            nname = "norm_mix" if wh == 0 else "norm_ffn"
            for c in range(8):
                TS(Amod[:, par, wh, c, :], mod[:, par, m * 8 + c, :], 1.0, cvcol(nname, i * 8 + c),
                   ALU.add, ALU.mult, r=[MODB[par][m], bCOLV], w=[AMOD[par][wh]])

    def ada_pop(n=1):
        for _ in range(n):
            i, t = ada_state["layer"], ada_state["next"]
            if i >= DEPTH:
                return
            ada_tile(i, t)
            if t == 23:
                ada_state["layer"], ada_state["next"] = i + 1, 0
            else:
                ada_state["next"] = t + 1

    def ada_need(i, m):
        while (ada_state["layer"], ada_state["next"]) <= (i, 4 * m + 3) and ada_state["layer"] < DEPTH:
            ada_pop()

    def ada_allow(i):
        return ada_state["layer"] <= i

    stg = [R0, R1]
    bstg = [bR0, [bR1]]
    if TOK >= 2048:
        for c_ in range(4):
            stg.append(wk[:, c_, 0:2048].bitcast(F32))
            bstg.append([WK[c_][b_] for b_ in range(NPB)])
    NSTG = len(stg)
    xl = {"n": 0}

    def xload_block(b, co=None):
        if b == SB:
            q = xl["n"] % NSTG
            xl["n"] += 1
            DMA("sync", stg[q][0:NS, :], xs[:, :], "xin%d" % q, w=bstg[q])
            bk = alloc_bank()
            for c in range(8):
                tr(ps[:, bk, c * 16:(c + 1) * 16], stg[q][0:NS, c * 128:(c + 1) * 128], ident[0:NS, 0:NS],
                   r=bstg[q] + [bCST], w=[PSB[bk]])
            VCOPY(xT[:, :, SEQ:SEQ + NS], ps[:, bk, 0:128].rearrange("p (c n) -> p c n", c=8),
                  r=[PSB[bk]], w=[X[c][SB] for c in range(8)])
            if co is not None:
                for c in range(-1, 8):
                    co(c)
            return
        if co is not None:
            co(-1)
        for ti, tt in enumerate(range(4 * b, 4 * b + 4)):
            q = xl["n"] % NSTG
            xl["n"] += 1
            DMA("sync", stg[q], xp[tt * 128:(tt + 1) * 128, :], "xin%d" % q, w=bstg[q])
            bk = alloc_bank(2)
            for c in range(8):
                tr(ps[:, bk + c // 4, (c % 4) * 128:(c % 4 + 1) * 128], stg[q][:, c * 128:(c + 1) * 128],
                   ident, r=bstg[q] + [bCST], w=[PSB[bk + c // 4]])
            for hh in range(2):
                o_ap = xT[:, 4 * hh:4 * hh + 4, tt * 128:(tt + 1) * 128]
                i_ap = ps[:, bk + hh, :].rearrange("p (c n) -> p c n", c=4)
                wb = [X[c][b] for c in range(4 * hh, 4 * hh + 4)]
                if hh == 0:
                    ACOPY(o_ap, i_ap, r=[PSB[bk + hh]], w=wb)
                else:
                    VCOPY(o_ap, i_ap, r=[PSB[bk + hh]], w=wb)
            if co is not None:
                co(2 * ti)
                co(2 * ti + 1)

    nst = {}

    def norm_sq(key, b, c):
        col0, n = blocks[b]
        q = nxt("sq", 3)
        ACT(sq[:, q, 0:n], xT[:, c, col0:col0 + n], AF.Square, r=[X[c][b]], w=[SQ[q]])
        nst[(key, b, "q", c)] = q

    def norm_a(key, b, c):
        col0, n = blocks[b]
        if c == -1:
            bk = alloc_bank()
            nst[(key, b)] = bk
            held_banks.add(bk)
            norm_sq(key, b, 0)
            return
        bk = nst[(key, b)]
        q = nst.pop((key, b, "q", c))
        mm(ps[:, bk, 0:n], ones_m[:, :], sq[:, q, 0:n], c == 0, c == 7, r=[SQ[q], bONES], w=[PSB[bk]])
        if c < 7:
            norm_sq(key, b, c + 1)

    def norm_rstd(key, b):
        col0, n = blocks[b]
        bk = nst.pop((key, b))
        held_banks.discard(bk)
        rq = nxt("rq", 2)
        ACT(rstd[:, rq, 0:n], ps[:, bk, 0:n], AF.Ln, r=[PSB[bk], bONES], w=[RSTD[rq]], bias=epsc[:, :], scale=1.0)
        ACT(rstd[:, rq, 0:n], rstd[:, rq, 0:n], AF.Exp, r=[RSTD[rq]], w=[RSTD[rq]], scale=-0.5)
        return rq

    def norm_b(i, wh, pool_layer, b, c):
        par = i % 2
        col0, n = blocks[b]
        if c == -1:
            ada_need(i, 3 * wh + 1)
            nst[("rq", i, wh, b)] = norm_rstd((i, wh), b)
            return
        rq = nst[("rq", i, wh, b)]
        if c == 7:
            nst.pop(("rq", i, wh, b))
        MSH = MODB[par][3 * wh]
        tq = nxt("tq", 3)
        A = Amod[:, par, wh, c, :]
        sh = mod[:, par, (3 * wh) * 8 + c, :]
        if b != SB:
            STT(tmp[:, tq, 0:n], xT[:, c, col0:col0 + n], A[:, 0:1], rstd[:, rq, 0:n], ALU.mult, ALU.mult,
                r=[X[c][b], AMOD[par][wh], RSTD[rq]], w=[TMP[tq]])
            ACT(hT[:, c, 16 + col0:16 + col0 + n], tmp[:, tq, 0:n], AF.Identity,
                r=[TMP[tq], MSH], w=[H[c][b]], bias=sh[:, 0:1], scale=1.0)
            if pool_layer and wh == 0 and b == NPB - 1:
                ACT(hlast[:, c, :], tmp[:, tq, n - 16:n], AF.Identity,
                    r=[TMP[tq], MSH], w=[bHLAST], bias=sh[:, 0:1], scale=1.0)
        else:
            TT(tmp[:, tq, 0:NS], xT[:, c, col0:col0 + NS], rstd[:, rq, 0:NS], ALU.mult,
               r=[X[c][b], RSTD[rq]], w=[TMP[tq]])
            TT(tmp[:, tq, 0:NS], tmp[:, tq, 0:NS], A[:, 1:17], ALU.mult,
               r=[TMP[tq], AMOD[par][wh]], w=[TMP[tq]])
            TT(hs32[:, c, :], tmp[:, tq, 0:NS], sh[:, 1:17], ALU.add,
               r=[TMP[tq], MSH], w=[bHS32])
            ACOPY(hT[:, c, 16 + SEQ:16 + SEQ + NS], hs32[:, c, :], r=[bHS32], w=[H[c][b]])

    def x_update(bk, oc, b, par, m):
        col0, n = blocks[b]
        g = mod[:, par, m * 8 + oc, :]
        if b != SB:
            STT(xT[:, oc, col0:col0 + n], ps[:, bk, 0:n], g[:, 0:1], xT[:, oc, col0:col0 + n], ALU.mult, ALU.add,
                r=[PSB[bk], MODB[par][m], X[oc][b]], w=[X[oc][b]])
        else:
            tb = nxt("t16b", 2)
            TT(t16b[:, tb, :], ps[:, bk, 0:NS], g[:, 1:17], ALU.mult, r=[PSB[bk], MODB[par][m]], w=[bT16B])
            TT(xT[:, oc, col0:col0 + NS], t16b[:, tb, :], xT[:, oc, col0:col0 + NS], ALU.add,
               r=[bT16B, X[oc][b]], w=[X[oc][b]])

    def make_outproj(i, W2d, src, SRC, m):
        par = i % 2
        st_ = {"tiles": None, "done": 0}

        def prefetch():
            if st_["tiles"] is None:
                ada_need(i, m)
                st_["tiles"] = [wget_cols(W2d, t * 256) for t in range(4)]

        def stage(b, co=None):
            prefetch()
            col0, n = blocks[b]
            if co is not None:
                co(-1)
            for oc in range(8):
                s, wv = st_["tiles"][oc // 2]
                oc2 = oc % 2
                bk = alloc_bank()
                for k in range(8):
                    mm(ps[:, bk, 0:n], wv[:, k, oc2 * 128:(oc2 + 1) * 128], src(k, col0, n), k == 0, k == 7,
                       r=[SLOT[s], SRC[k][b]], w=[PSB[bk]])
                if co is not None:
                    co(oc)
                x_update(bk, oc, b, par, m)
            st_["done"] += 1
            if st_["done"] == NB:
                for s, _ in st_["tiles"]:
                    wrel(s)
        stage.prefetch = prefetch
        return stage

    def tok_major_out(srcT, bsrc, dst_ap_fn, stage, bstage, sem):
        bk = alloc_bank(2)
        for c in range(8):
            tr(ps[0:16, bk + c // 4, (c % 4) * 128:(c % 4 + 1) * 128], srcT[:, c, :], ident,
               r=[bsrc, bCST], w=[PSB[bk + c // 4]])
        VCOPY(stage[0:16, :].rearrange("p (a n) -> p a n", a=2), ps[0:16, bk:bk + 2, :],
              r=[PSB[bk], PSB[bk + 1]], w=bstage)
        o_ap, i_ap = dst_ap_fn(stage)
        DMA("sync", o_ap, i_ap, sem, r=bstage)

    def make_pool(i):
        par = i % 2
        j = i // 2
        zt = [R2[:, 0:512].bitcast(BF16), R2[:, 512:1024].bitcast(BF16), R3.bitcast(BF16)]
        bzt = ZTH
        FENCE(ALIAS_R23)
        gst = {"slot": None, "gv": None, "done": 0, "zi": 0, "pend": None, "prev": None}

        def get_w():
            if gst["slot"] is None:
                s = slot_alloc()
                gv = wsl[:, s, :].rearrange("p (g k n) -> p g k n", g=4, k=2)
                for g in range(4):
                    DMA("gpsimd", gv[:, g, :, :], pool_w_grp[j, g].rearrange("(k p) n -> p k n", p=128),
                        "w%d" % s, w=[SLOT[s]])
                DMA("sync", R1, pool_scale[j].partition_broadcast(128), "pscale", w=[bR1])
                for k in range(2):
                    TT(gv[:, :, k, :], gv[:, :, k, :], R1.rearrange("p (g n) -> p g n", g=4), ALU.mult,
                       r=[SLOT[s], bR1], w=[SLOT[s]])
                gst["slot"], gst["gv"] = s, gv
            return gst["slot"], gst["gv"]

        def zA(tt):
            s, gv = get_w()
            b = tt // 4
            zi = gst["zi"] % 3
            gst["zi"] += 1
            bz = alloc_bank(2)
            for g in range(4):
                for k in range(2):
                    mm(ps[:, bz + g // 2, (g % 2) * 256:(g % 2 + 1) * 256],
                       hT[:, 2 * g + k, 16 + tt * 128:16 + (tt + 1) * 128], gv[:, g, k, :], k == 0, k == 1,
                       r=[SLOT[s], H[2 * g + k][b]], w=[PSB[bz + g // 2]])
            ACOPY(zt[zi][:, 0:512], ps[:, bz, :], r=[PSB[bz]], w=[bzt[zi][0]])
            VCOPY(zt[zi][:, 512:1024], ps[:, bz + 1, :], r=[PSB[bz + 1]], w=[bzt[zi][1]])
            return zi

        def zB(tt, zi, zprev):
            b = tt // 4
            for quad in range(2):
                bp = alloc_bank()
                for ci in range(4):
                    c = 4 * quad + ci
                    wi = c // 2
                    kind = 2 if tt == 0 else 0
                    pa = pmat[:, (kind * 4 + wi) * 128:(kind * 4 + wi + 1) * 128]
                    mm(ps[:, bp, ci * 128:(ci + 1) * 128], zt[zi][:, c * 128:(c + 1) * 128], pa, True, tt == 0,
                       r=[bzt[zi][quad], bPMAT], w=[PSB[bp]])
                    if tt > 0:
                        pb = pmat[:, (4 + wi) * 128:(4 + wi) * 128 + 16]
                        mm(ps[:, bp, ci * 128:ci * 128 + 16], zt[zprev][:, c * 128:(c + 1) * 128], pb, False, True,
                           r=[bzt[zprev][quad], bPMAT], w=[PSB[bp]])
                VCOPY(wk[:, 4 * quad:4 * quad + 4, tt * 128:(tt + 1) * 128],
                      ps[:, bp, :].rearrange("p (c n) -> p c n", c=4),
                      r=[PSB[bp]], w=[WK[c][b] for c in range(4 * quad, 4 * quad + 4)])

        HPRE = (TOK >= 2048)
        if HPRE:
            hbuf = [wk[:, 6 + t_, 0:2048].bitcast(F32) for t_ in range(2)]
            hb = [[WK[6 + t_][b_] for b_ in range(NB)] for t_ in range(2)]
            for t_ in range(2):
                DMA("sync", hbuf[t_][0:120, :], sp[j, 8 * t_:8 * t_ + 8].rearrange("b t d -> (b t) d"),
                    "hist%d" % t_, w=hb[t_])

        def zpool(b):
            if b == SB:
                s, gv = get_w()
                bk = alloc_bank()
                for tile in range(2):
                    if HPRE:
                        hsrc, hbufs = hbuf[tile], hb[tile]
                    else:
                        DMA("sync", R1[0:120, :], sp[j, 8 * tile:8 * tile + 8].rearrange("b t d -> (b t) d"), "hist",
                            w=[bR1])
                        hsrc, hbufs = R1, [bR1]
                    for c in range(8):
                        wi = c // 2
                        mm(ps[:, bk, c * 16 + 8 * tile:c * 16 + 8 * tile + 8], hsrc[0:120, c * 128:(c + 1) * 128],
                           sel[0:120, wi * 8:(wi + 1) * 8], True, True, r=hbufs + [bCST], w=[PSB[bk]])
                for c in range(8):
                    w = WINDOWS[c // 2]
                    TS(t16[:, c, :], hs32[:, c, :], (1.0 / w - 1.0), None, ALU.mult, None, r=[bHS32], w=[bT16])
                    STT(pl16[:, c, :], ps[:, bk, c * 16:(c + 1) * 16], 1.0 / w, t16[:, c, :],
                        ALU.mult, ALU.add, r=[PSB[bk], bT16], w=[bPL16])
                tok_major_out(hs32, bHS32, lambda stage: (nps[j, :, PB - 1, :], stage[0:16, :]), R1, [bR1], "out_ns")
                bk = alloc_bank()
                for g in range(4):
                    for oc in range(2):
                        ch = 2 * g + oc
                        for k in range(2):
                            mm(ps[:, bk, ch * 16:(ch + 1) * 16], gv[:, g, k, oc * 128:(oc + 1) * 128],
                               pl16[:, 2 * g + k, :], k == 0, k == 1, r=[SLOT[s], bPL16], w=[PSB[bk]])
                VCOPY(wk[:, :, SEQ:SEQ + NS], ps[:, bk, 0:128].rearrange("p (c n) -> p c n", c=8),
                      r=[PSB[bk]], w=[WK[c][SB] for c in range(8)])
            else:
                for tt in range(4 * b, 4 * b + 4):
                    zi = zA(tt)
                    if gst["pend"] is not None:
                        zB(*gst["pend"])
                    gst["pend"] = (tt, zi, gst["prev"])
                    gst["prev"] = zi
                if b == NPB - 1:
                    zB(*gst["pend"])
                    gst["pend"] = None
                    tok_major_out(hlast, bHLAST, lambda stage: (npp[j, :, :], stage[1:16, :]), R1, [bR1], "out_np")
            gst["done"] += 1
            if gst["done"] == NB:
                wrel(gst["slot"])

        outp = make_outproj(i, pool_w_out[j], lambda k, col0, n: wk[:, k, col0:col0 + n], WK, 2)
        return zpool, outp

    def gmlp_prep_parts(i):
        j = i // 2
        wss = R0.rearrange("p (g s) -> p g s", g=8)
        Cb = R1
        WsT = R3.bitcast(BF16).rearrange("p (g t) -> p g t", g=8)

        def part0():
            FENCE(ALIAS_R23)
            DMA("sync", wss, gmlp_w_s[j].rearrange("g t s -> t g s"), "wss", w=bR0)
            DMA("sync", Cb, gmlp_b_s[j].rearrange("g t -> (g t)").partition_broadcast(128), "cb", w=[bR1])

        def part1():
            VCOPY(bs0[:, :], R1[:, 0:1024:128], r=[bR1], w=[bBS0])
            bk = alloc_bank()
            mm(ps[:, bk, 0:8], E0, R0[:, 0:1024:128], True, True, r=bR0 + [bCST], w=[PSB[bk]])
            VCOPY(ws00[:, :], ps[:, bk, 0:8], r=[PSB[bk]], w=[bWS00])
            for half in range(2):
                bk = alloc_bank()
                for gl in range(4):
                    g = 4 * half + gl
                    tr(ps[:, bk, gl * 128:(gl + 1) * 128], wss[:, g, :], ident, r=bR0 + [bCST], w=[PSB[bk]])
                for gl in range(4):
                    g = 4 * half + gl
                    TT(WsT[:, g, :], ps[:, bk, gl * 128:(gl + 1) * 128], maskT, ALU.mult,
                       r=[PSB[bk], bCST], w=[bR3])

        def part2():
            for half in range(2):
                bk = alloc_bank()
                for gl in range(4):
                    g = 4 * half + gl
                    mm(ps[:, bk, gl * 128:(gl + 1) * 128], ones_1[:, :], WsT[:, g, :], True, True,
                       r=[bONES, bR3], w=[PSB[bk]])
                for gl in range(4):
                    g = 4 * half + gl
                    STT(Cb[:, g * 128:(g + 1) * 128], ps[:, bk, gl * 128:(gl + 1) * 128],
                        cvcol("gmlp_ln_b", j * 8 + g), Cb[:, g * 128:(g + 1) * 128], ALU.mult, ALU.add,
                        r=[PSB[bk], bCOLV, bR1], w=[bR1])
        return [part0, part1, part2]

    def ln_stats(bk, npart, vq):
        o = vq * 16
        bsm = SMALLB[vq]
        P.op("vector", lambda e: e.bn_stats(out=small[0:npart, o:o + 6], in_=ps[0:npart, bk, :]),
             r=[PSB[bk]], w=[bsm])
        P.op("vector", lambda e: e.bn_stats(out=small[0:npart, o + 6:o + 12], in_=ps[0:npart, bk + 1, :]),
             r=[PSB[bk + 1]], w=[bsm])
        P.op("vector", lambda e: e.bn_aggr(out=small[0:npart, o + 12:o + 14], in_=small[0:npart, o:o + 12]),
             r=[bsm], w=[bsm])
        rsqrt_act(small[0:npart, o + 13:o + 14], small[0:npart, o + 13:o + 14], npart, r=[bsm], w=[bsm])
        TS(small[0:npart, o + 14:o + 15], small[0:npart, o + 12:o + 13], small[0:npart, o + 13:o + 14], -1.0,
           ALU.mult, ALU.mult, r=[bsm], w=[bsm])
        return small[0:npart, o + 13:o + 14], small[0:npart, o + 14:o + 15]

    def make_gmlp(i):
        par = i % 2
        j = i // 2
        Cb = R1
        WsT = R3.bitcast(BF16).rearrange("p (g t) -> p g t", g=8)
        vt = [R2[:, 0:512].bitcast(BF16), R2[:, 512:1024].bitcast(BF16), arena[:, 0:512].bitcast(BF16)]
        bvt = [bR2a, bR2b, bR0a]
        gs = {"vs": None, "pend": [], "cnt": 2, "samp": False}

        def vA(tt):
            if gs["vs"] is None:
                gs["vs"] = [wget_cols(gmlp_w_in[j], 1024 + t * 256) for t in range(4)]
            samp = tt == NTT
            npart = NS if samp else 128
            hc0 = 16 + tt * 128
            b = SB if samp else tt // 4
            bk = alloc_bank(2)
            for t in range(4):
                s, wv = gs["vs"][t]
                for k in range(8):
                    mm(ps[0:npart, bk + t // 2, (t % 2) * 256:(t % 2 + 1) * 256], hT[:, k, hc0:hc0 + npart],
                       wv[:, k, :], k == 0, k == 7, r=[SLOT[s], H[k][b]], w=[PSB[bk + t // 2]])
            if not samp:
                vq = gs["cnt"] % 3
                gs["cnt"] += 1
                rs_ap, nb_ap = ln_stats(bk, npart, vq)
                ACT(vt[vq].rearrange("p (a n) -> p a n", a=2), ps[:, bk:bk + 2, :], AF.Identity,
                    r=[PSB[bk], PSB[bk + 1], SMALLB[vq]], w=[bvt[vq]], bias=nb_ap, scale=rs_ap)
            else:
                vq = 0
                rs_ap, nb_ap = ln_stats(bk, npart, vq)
                zs = R2[0:NS, :]
                ACT(zs.rearrange("p (a n) -> p a n", a=2), ps[0:NS, bk:bk + 2, :], AF.Identity,
                    r=[PSB[bk], PSB[bk + 1], SMALLB[vq]], w=bR2, bias=nb_ap, scale=rs_ap)
            return vq

        def vB(tt, vq):
            samp = tt == NTT
            if not samp:
                b = tt // 4
                for half in range(2):
                    bs_ = alloc_bank()
                    for gl in range(4):
                        g = 4 * half + gl
                        mm(ps[:, bs_, gl * 128:(gl + 1) * 128], vt[vq][:, g * 128:(g + 1) * 128],
                           WsT[:, g, :], True, True, r=[bvt[vq], bR3], w=[PSB[bs_]])
                    for gl in range(4):
                        g = 4 * half + gl
                        STT(wk[:, g, tt * 128:(tt + 1) * 128], ps[:, bs_, gl * 128:(gl + 1) * 128],
                            cvcol("gmlp_ln_g", j * 8 + g), Cb[:, g * 128:(g + 1) * 128], ALU.mult, ALU.add,
                            r=[PSB[bs_], bCOLV, bR1], w=[WK[g][b]])
            else:
                bz = alloc_bank()
                for g in range(8):
                    tr(ps[:, bz, g * 16:(g + 1) * 16], R2[0:NS, g * 128:(g + 1) * 128], ident[0:NS, 0:NS],
                       r=bR2 + [bCST], w=[PSB[bz]])
                for g in range(8):
                    TS(vnT[:, g, :], ps[:, bz, g * 16:(g + 1) * 16], cvcol("gmlp_ln_g", j * 8 + g),
                       cvcol("gmlp_ln_b", j * 8 + g), ALU.mult, ALU.add, r=[PSB[bz], bCOLV], w=[bVNT])
                tok_major_out(vnT, bVNT, lambda stage: (ncv[j, :, :], stage[0:16, :]), R2, bR2, "out_v")
                for g in range(8):
                    TS(wk[:, g, SEQ:SEQ + NS], vnT[:, g, :], ws00[:, g:g + 1], bs0[:, g:g + 1], ALU.mult, ALU.add,
                       r=[bVNT, bWS00, bBS0], w=[WK[g][SB]])

        def v_stage(b):
            if b == SB:
                vA(NTT)
                gs["samp"] = True
                return
            for tt in range(4 * b, 4 * b + 4):
                vq = vA(tt)
                if gs["samp"]:
                    vB(NTT, 0)
                    gs["samp"] = False
                gs["pend"].append((tt, vq))
                if len(gs["pend"]) > 2:
                    vB(*gs["pend"].pop(0))

        def v_flush():
            assert not gs["samp"]
            while gs["pend"]:
                vB(*gs["pend"].pop(0))
            for s, _ in gs["vs"]:
                wrel(s)

        def u_phase():
            for t in range(4):
                s, wv = wget_cols(gmlp_w_in[j], t * 256)
                for oc2 in range(2):
                    oc = 2 * t + oc2
                    for b in ORDER:
                        col0, n = blocks[b]
                        bk = alloc_bank()
                        for k in range(8):
                            mm(ps[:, bk, 0:n], wv[:, k, oc2 * 128:(oc2 + 1) * 128],
                               hT[:, k, 16 + col0:16 + col0 + n], k == 0, k == 7,
                               r=[SLOT[s], H[k][b]], w=[PSB[bk]])
                        TT(wk[:, oc, col0:col0 + n], ps[:, bk, 0:n], wk[:, oc, col0:col0 + n], ALU.mult,
                           r=[PSB[bk], WK[oc][b]], w=[WK[oc][b]])
                wrel(s)
                if ada_allow(i + 1):
                    ada_pop(1)

        outp = make_outproj(i, gmlp_w_out[j], lambda k, col0, n: wk[:, k, col0:col0 + n], WK, 2)
        return v_stage, v_flush, u_phase, outp

    def make_ffn(i):
        par = i % 2
        groups = [[0, 1, 2, 3], [4, 5, 6, 7], [8, 9, 10]]

        def gate_up(grp):
            for ti, t in enumerate(grp):
                sg, gvw = wget_cols(ffn_w_gate[i], t * 256)
                su, uvw = wget_cols(ffn_w_up[i], t * 256)
                for fc in range(2):
                    wc = 2 * ti + fc
                    for b in ORDER:
                        col0, n = blocks[b]
                        bg = alloc_bank()
                        for k in range(8):
                            mm(ps[:, bg, 0:n], gvw[:, k, fc * 128:(fc + 1) * 128], hT[:, k, 16 + col0:16 + col0 + n],
                               k == 0, k == 7, r=[SLOT[sg], H[k][b]], w=[PSB[bg]])
                        bu = alloc_bank()
                        for k in range(8):
                            mm(ps[:, bu, 0:n], uvw[:, k, fc * 128:(fc + 1) * 128], hT[:, k, 16 + col0:16 + col0 + n],
                               k == 0, k == 7, r=[SLOT[su], H[k][b]], w=[PSB[bu]])
                        q = nxt("sq", 3)
                        ACT(sq[:, q, 0:n], ps[:, bg, 0:n], AF.Silu, r=[PSB[bg]], w=[SQ[q]])
                        TT(wk[:, wc, col0:col0 + n], ps[:, bu, 0:n], sq[:, q, 0:n], ALU.mult,
                           r=[PSB[bu], SQ[q]], w=[WK[wc][b]])
                wrel(sg)
                wrel(su)
                if ada_allow(i + 1):
                    ada_pop(2)

        def make_down(grp):
            st_ = {"sd": None, "done": 0}
            nk = 2 * len(grp)

            def stage(b, co=None):
                if st_["sd"] is None:
                    ada_need(i, 5)
                    st_["sd"] = [wget_rows(ffn_w_down[i], t * 256) for t in grp]
                col0, n = blocks[b]
                if co is not None:
                    co(-1)
                for oc in range(8):
                    bk = alloc_bank()
                    for kk in range(nk):
                        s, dv = st_["sd"][kk // 2]
                        mm(ps[:, bk, 0:n], dv[:, kk % 2, oc * 128:(oc + 1) * 128], wk[:, kk, col0:col0 + n],
                           kk == 0, kk == nk - 1, r=[SLOT[s], WK[kk][b]], w=[PSB[bk]])
                    if co is not None:
                        co(oc)
                    x_update(bk, oc, b, par, 5)
                st_["done"] += 1
                if st_["done"] == NB:
                    for s, _ in st_["sd"]:
                        wrel(s)
            return stage

        def front(hooks=()):
            hooks = list(hooks)
            if hooks:
                hooks.pop(0)()
            for gi, grp in enumerate(groups):
                gate_up(grp)
                if hooks:
                    hooks.pop(0)()
                if gi < len(groups) - 1:
                    dn = make_down(grp)
                    for b in ORDER:
                        dn(b)
                    if ada_allow(i + 1):
                        ada_pop(1)
            while hooks:
                hooks.pop(0)()
            return make_down(groups[-1])
        return front

    fin = {"ot": 0}
    ostg = [R0, R1]
    bostg = [bR0, [bR1]]

    gfb = tmp[:, 0:2, :].rearrange("p a n -> p (a n)")

    def final_prep():
        DMA("sync", gfb, norm_final.partition_broadcast(128), "gfb", w=[TMP[0], TMP[1]])

    def final_b(b):
        col0, n = blocks[b]
        nsub = 1 if b == SB else n // 128
        npart = NS if b == SB else 128
        for s_ in range(nsub):
            c0 = col0 + s_ * 128
            bb = alloc_bank(2)
            for c in range(8):
                tr(ps[0:npart, bb + c // 4, (c % 4) * 128:(c % 4 + 1) * 128], xT[:, c, c0:c0 + npart], ident,
                   r=[X[c][b], bCST], w=[PSB[bb + c // 4]])
            vq = fin["ot"] % 3
            o = vq * 16
            bsm = SMALLB[vq]
            P.op("vector", lambda e, o=o, bb=bb: e.bn_stats(out=small[0:npart, o:o + 6], in_=ps[0:npart, bb, :]),
                 r=[PSB[bb]], w=[bsm])
            P.op("vector", lambda e, o=o, bb=bb: e.bn_stats(out=small[0:npart, o + 6:o + 12],
                                                             in_=ps[0:npart, bb + 1, :]),
                 r=[PSB[bb + 1]], w=[bsm])
            P.op("vector", lambda e, o=o: e.bn_aggr(out=small[0:npart, o + 12:o + 14], in_=small[0:npart, o:o + 12]),
                 r=[bsm], w=[bsm])
            STT(small[0:npart, o + 14:o + 15], small[0:npart, o + 12:o + 13], small[0:npart, o + 12:o + 13],
                small[0:npart, o + 13:o + 14], ALU.mult, ALU.add, r=[bsm], w=[bsm])
            rsqrt_act(small[0:npart, o + 14:o + 15], small[0:npart, o + 14:o + 15], npart, r=[bsm], w=[bsm])
            q = fin["ot"] % 2
            fin["ot"] += 1
            ACT(ostg[q][0:npart, :].rearrange("p (a n) -> p a n", a=2), ps[0:npart, bb:bb + 2, :], AF.Identity,
                r=[PSB[bb], PSB[bb + 1], bsm], w=bostg[q], scale=small[0:npart, o + 14:o + 15])
            P.op("gpsimd", lambda e, q=q: e.tensor_tensor(out=ostg[q][0:npart, :], in0=ostg[q][0:npart, :],
                                                            in1=gfb[0:npart, :], op=ALU.mult),
                 r=bostg[q] + [TMP[0], TMP[1]], w=bostg[q])
            if b == SB:
                DMA("sync", ys[:, :], ostg[q][0:NS, :], "out_y%d" % q, r=bostg[q])
            else:
                DMA("sync", yp[c0:c0 + 128, :], ostg[q], "out_y%d" % q, r=bostg[q])

    ada_need(0, 1)
    DMA("gpsimd", pmat[:, :], pmat_d[:, :], "pm", w=[bPMAT])
    prev_stage = xload_block
    for i in range(DEPTH):
        pool_layer = (i % 2 == 0)
        n1a = (lambda b, c, i=i: norm_a((i, 0), b, c))
        n1b = (lambda b, c, i=i, pl=pool_layer: norm_b(i, 0, pl, b, c))
        n2a = (lambda b, c, i=i: norm_a((i, 1), b, c))
        n2b = (lambda b, c, i=i, pl=pool_layer: norm_b(i, 1, pl, b, c))

        stepc = {"n": 0}

        def step_pop(i=i, stepc=stepc):
            if ada_allow(i):
                if i == 0:
                    ada_pop(1 if stepc["n"] < 5 else 3)
                else:
                    ada_pop(2)
            stepc["n"] += 1

        if pool_layer:
            zpool, outp = make_pool(i)
            pipeline([(prev_stage, 0, [(n1a, 1), (n1b, 2)]), (zpool, 3), (outp, 4, [(n2a, 5), (n2b, 6)])],
                     ORDER, per_step=step_pop)
        else:
            v_stage, v_flush, u_phase, outp = make_gmlp(i)
            pipeline([(prev_stage, 0, [(n1a, 1), (n1b, 2)]), (v_stage, 3)], ORDER, per_step=step_pop)
            v_flush()
            outp.prefetch()
            u_phase()
            pipeline([(outp, 0, [(n2a, 1), (n2b, 2)])], ORDER, per_step=step_pop)
        ada_need(i, 5)
        if i == 0:
            for j in range(NPOOL):
                DMA("sync", nps[j, :, 0:PB - 1, :], sp[j, :, 1:PB, :], "out_hist")
        nxt_prep = gmlp_prep_parts(i + 1) if (i + 1 < DEPTH and (i + 1) % 2 == 1) else []
        prev_stage = make_ffn(i)(nxt_prep)
    final_prep()
    pipeline([(prev_stage, 0), (final_b, 1)], ORDER)

    finals = ["out_hist", "out_np", "out_ns", "out_v", "out_y0", "out_y1"]
    stats = P.emit(nc, st, finals)
    st.close()
    return nc, stats


def make_cst():
    c = np.zeros((128, NCST), np.float32)
    c[:, 0:128] = np.eye(128, dtype=np.float32)
    s_ = np.arange(128)[:, None]
    t_ = np.arange(128)[None, :]
    c[:, 128:256] = (s_ <= t_).astype(np.float32)
    for wi, w in enumerate(WINDOWS):
        for t in range(16):
            c[:, 256 + wi * 16 + t] = 1.0 / min(t + 1, w)
        for r in range(120):
            bl, tt = divmod(r, PB)
            if tt >= 16 - w:
                c[r, 320 + wi * 8 + bl] = 1.0
    c[0, 352:480] = 1.0
    return c


def make_pmat():
    m = np.zeros((128, 12 * 128), np.float32)
    s_ = np.arange(128)[:, None]
    t_ = np.arange(128)[None, :]
    for wi, w in enumerate(WINDOWS):
        pa = ((s_ <= t_) & (s_ > t_ - w)).astype(np.float32) / w - (s_ == t_).astype(np.float32)
        pb = (s_ >= 129 + t_ - w).astype(np.float32) / w
        n_t = np.minimum(t_ + 1, w).astype(np.float32)
        pa0 = ((s_ <= t_) & (s_ > t_ - w)).astype(np.float32) / n_t - (s_ == t_).astype(np.float32)
        m[:, (0 + wi) * 128:(0 + wi + 1) * 128] = pa
        m[:, (4 + wi) * 128:(4 + wi + 1) * 128] = pb
        m[:, (8 + wi) * 128:(8 + wi + 1) * 128] = pa0
    return m


_WNAMES = ("w_ada", "b_ada", "norm_mix", "norm_ffn", "norm_final", "pool_w_grp", "pool_scale", "pool_w_out",
           "gmlp_w_in", "gmlp_ln_g", "gmlp_ln_b", "gmlp_w_s", "gmlp_b_s", "gmlp_w_out",
           "ffn_w_gate", "ffn_w_up", "ffn_w_down")

_CACHE = {}


def run(inputs, cfg, ncores):
    key = (cfg.SEQ, cfg.DEPTH)
    if key not in _CACHE:
        _CACHE[key] = build_program(cfg)
    nc, _ = _CACHE[key]
    f = lambda a: np.ascontiguousarray(np.asarray(a, dtype=np.float32))
    wts = {n: f(inputs[n]) for n in _WNAMES}
    cstv = make_cst()
    pmatv = make_pmat()
    xpr, xsm, spl = f(inputs["x_prompt"]), f(inputs["x_sample"]), f(inputs["state_pool"])
    cp, cs = f(inputs["c_prompt"]), f(inputs["c_sample"])
    in_maps = []
    for r in range(ncores):
        m = dict(wts)
        m["xp"] = xpr[r]
        m["xs"] = np.ascontiguousarray(xsm[NS * r:NS * (r + 1), 0, :])
        m["sp"] = np.ascontiguousarray(spl[:, NS * r:NS * (r + 1)])
        m["cc"] = np.ascontiguousarray(np.concatenate([cp[r:r + 1], cs[NS * r:NS * (r + 1)]], axis=0))
        m["cst"] = cstv
        m["pmat"] = pmatv
        in_maps.append(m)
    res = run_bass_kernel_spmd(nc, in_maps, core_ids=list(range(ncores)))
    rs = res.results
    y_prompt = np.stack([rs[r]["yp"] for r in range(ncores)], axis=0)
    y_sample = np.concatenate([rs[r]["ys"] for r in range(ncores)], axis=0)[:, None, :]
    npp = np.stack([rs[r]["npp"] for r in range(ncores)], axis=1)
    nps = np.concatenate([rs[r]["nps"] for r in range(ncores)], axis=1)
    ncv = np.concatenate([rs[r]["ncv"] for r in range(ncores)], axis=1)[:, :, None, :]
    return (y_prompt.astype(np.float32), y_sample.astype(np.float32), npp.astype(np.float32),
            nps.astype(np.float32), ncv.astype(np.float32))


def kernel(**inputs):
    return run(inputs, Cfg(2048, 4), NCORES)
```

```python
import numpy as np
from contextlib import ExitStack

import concourse.bass as bass
import concourse.mybir as mybir
from concourse.bass_utils import run_bass_kernel_spmd

F32 = mybir.dt.float32
BF16 = mybir.dt.bfloat16
AF = mybir.ActivationFunctionType
ALU = mybir.AluOpType

D = 1024
NCH = 8
FF = 2816
NFT = 11
WINDOWS = (2, 4, 8, 16)
PB = 15
EPS = 1e-6
NS = 16
NCORES = 8
NSLOT = 8
NCST = 480

ENGS = ("tensor", "scalar", "vector", "gpsimd", "sync")


class Buf:
    __slots__ = ("key", "w", "r")

    def __init__(self, key):
        self.key = key
        self.w = {}
        self.r = {}


class Op:
    __slots__ = ("fn", "deps", "dma")

    def __init__(self, fn, deps, dma):
        self.fn = fn
        self.deps = deps
        self.dma = dma


class Plan:
    def __init__(self):
        self.ops = {e: [] for e in ENGS}
        self.dmacnt = {}
        self.bufs = {}
        self.final_sems = set()

    def buf(self, *key):
        b = self.bufs.get(key)
        if b is None:
            b = self.bufs[key] = Buf(key)
        return b

    def op(self, eng, fn, r=(), w=(), dma=None):
        idx = len(self.ops[eng])
        deps = {}

        def add(tok):
            k = (tok[0], tok[1])
            old = deps.get(k)
            if old is None or (tok[2] is not None and old[2] is not None and tok[2] > old[2]):
                deps[k] = tok

        is_dma = dma is not None
        for b in r:
            for tok in b.w.values():
                add(tok)
        for b in w:
            for tok in b.w.values():
                if is_dma and tok[0] == "d" and tok[1] == dma:
                    continue
                if is_dma or tok[0] == "d" or tok[1] != eng or eng != "tensor":
                    add(tok)
            for tok in b.r.values():
                if is_dma or tok[0] == "d" or tok[1] != eng or eng != "tensor":
                    add(tok)
        if is_dma:
            self.dmacnt[dma] = self.dmacnt.get(dma, 0) + 16
            tok = ("d", dma, None if dma in self.final_sems else self.dmacnt[dma])
        else:
            tok = ("e", eng, idx)
        for b in r:
            b.r[(tok[0], tok[1])] = tok
        for b in w:
            b.w[(tok[0], tok[1])] = tok
            b.r = {}
        self.ops[eng].append(Op(fn, list(deps.values()), dma))
        return tok

    def emit(self, nc, stack, final_waits):
        sig = {e: [False] * len(self.ops[e]) for e in ENGS}
        for e in ENGS:
            for o in self.ops[e]:
                for t in o.deps:
                    if t[0] == "e":
                        sig[t[1]][t[2]] = True
        cum = {}
        for e in ENGS:
            c = 0
            arr = []
            for i in range(len(self.ops[e])):
                if sig[e][i]:
                    c += 1
                arr.append(c)
            cum[e] = arr
        names = set()
        for e in ENGS:
            for i, o in enumerate(self.ops[e]):
                if o.dma is not None:
                    names.add("D_" + o.dma)
                elif sig[e][i]:
                    names.add("E_" + e)
        sems = {n: stack.enter_context(nc.semaphore(n)) for n in sorted(names)}
        stats = {}
        with nc.Block() as block:
            for e in ENGS:
                def body(engine, e=e):
                    waited = {}
                    nwait = 0
                    for i, o in enumerate(self.ops[e]):
                        for t in o.deps:
                            if t[0] == "e":
                                name = "E_" + t[1]
                                val = cum[t[1]][t[2]]
                            else:
                                name = "D_" + t[1]
                                val = t[2] if t[2] is not None else self.dmacnt[t[1]]
                            if waited.get(name, 0) < val:
                                engine.wait_ge(sems[name], val)
                                waited[name] = val
                                nwait += 1
                        ins = o.fn(engine)
                        if o.dma is not None:
                            ins.then_inc(sems["D_" + o.dma], 16)
                        elif sig[e][i]:
                            ins.then_inc(sems["E_" + e], 1)
                    if e == "sync":
                        for sname in final_waits:
                            if sname in self.dmacnt:
                                engine.wait_ge(sems["D_" + sname], self.dmacnt[sname])
                    stats[e] = (len(self.ops[e]), nwait)
                getattr(block, e)(body)
        return stats


class Cfg:
    def __init__(self, SEQ=2048, DEPTH=4):
        self.SEQ = SEQ
        self.DEPTH = DEPTH


def cv_layout(DEPTH):
    NPOOL = (DEPTH + 1) // 2
    NG = max(DEPTH // 2, 1)
    segs = [("b_ada", DEPTH * 48), ("norm_mix", DEPTH * 8), ("norm_ffn", DEPTH * 8),
            ("norm_final", 8), ("pool_scale", NPOOL * 8), ("gmlp_ln_g", NG * 8), ("gmlp_ln_b", NG * 8)]
    off = {}
    o = 0
    for n, k in segs:
        off[n] = o
        o += k
    return segs, off, o


def build_program(cfg):
    SEQ, DEPTH = cfg.SEQ, cfg.DEPTH
    NPB = SEQ // 512
    NTT = SEQ // 128
    TOK = SEQ + NS
    blocks = [(512 * b, 512) for b in range(NPB)] + [(SEQ, NS)]
    NB = len(blocks)
    SB = NB - 1
    NPOOL = (DEPTH + 1) // 2
    NG = DEPTH // 2
    NGA = max(NG, 1)

    nc = bass.Bass("TRN2", target_bir_lowering=False)

    def din(name, shape):
        return nc.dram_tensor(name, list(shape), F32, kind="ExternalInput").ap()

    def dout(name, shape):
        return nc.dram_tensor(name, list(shape), F32, kind="ExternalOutput").ap()

    xp = din("xp", [SEQ, D])
    xs = din("xs", [NS, D])
    sp = din("sp", [NPOOL, NS, PB, D])
    cc = din("cc", [1 + NS, D])
    cst = din("cst", [128, NCST])
    pmat_d = din("pmat", [128, 12 * 128])
    w_ada = din("w_ada", [DEPTH, D, 6 * D])
    b_ada = din("b_ada", [DEPTH, 6 * D])
    norm_mix = din("norm_mix", [DEPTH, D])
    norm_ffn = din("norm_ffn", [DEPTH, D])
    norm_final = din("norm_final", [D])
    pool_w_grp = din("pool_w_grp", [NPOOL, 4, 256, 256])
    pool_scale = din("pool_scale", [NPOOL, D])
    pool_w_out = din("pool_w_out", [NPOOL, D, D])
    gmlp_w_in = din("gmlp_w_in", [NGA, D, 2 * D])
    gmlp_ln_g = din("gmlp_ln_g", [NGA, D])
    gmlp_ln_b = din("gmlp_ln_b", [NGA, D])
    gmlp_w_s = din("gmlp_w_s", [NGA, 8, 128, 128])
    gmlp_b_s = din("gmlp_b_s", [NGA, 8, 128])
    gmlp_w_out = din("gmlp_w_out", [NGA, D, D])
    ffn_w_gate = din("ffn_w_gate", [DEPTH, D, FF])
    ffn_w_up = din("ffn_w_up", [DEPTH, D, FF])
    ffn_w_down = din("ffn_w_down", [DEPTH, FF, D])

    yp = dout("yp", [SEQ, D])
    ys = dout("ys", [NS, D])
    npp = dout("npp", [NPOOL, PB, D])
    nps = dout("nps", [NPOOL, NS, PB, D])
    ncv = dout("ncv", [NGA, NS, D])

    segs, cvoff, NCV = cv_layout(DEPTH)
    NCVT = (NCV + 127) // 128

    P = Plan()
    P.final_sems.add("init")
    st = ExitStack()

    def sb(name, shape, dt):
        return st.enter_context(nc.sbuf_tensor(name, list(shape), dt))

    xT = sb("xT", [128, 8, TOK], F32)
    hT = sb("hT", [128, 8, 16 + TOK], BF16)
    wk = sb("wk", [128, 8, TOK], BF16)
    wsl = sb("wsl", [128, NSLOT, 2048], BF16)
    mod = sb("mod", [128, 2, 48, 17], F32)
    Amod = sb("Amod", [128, 2, 2, 8, 17], F32)
    colv = sb("colv", [128, NCVT * 128], F32)
    cstt = sb("cstt", [128, NCST], F32)
    ones_m = sb("ones_m", [128, 128], BF16)
    ones_1 = sb("ones_1", [128, 128], BF16)
    scT = sb("scT", [128, 8, 17], BF16)
    rstd = sb("rstd", [128, 2, 512], F32)
    tmp = sb("tmp", [128, 3, 512], F32)
    sq = sb("sq", [128, 3, 512], BF16)
    hs32 = sb("hs32", [128, 8, 16], F32)
    hlast = sb("hlast", [128, 8, 16], F32)
    t16 = sb("t16", [128, 8, 16], F32)
    t16b = sb("t16b", [128, 2, 16], F32)
    small = sb("small", [128, 64], F32)
    ws00 = sb("ws00", [128, 8], F32)
    bs0 = sb("bs0", [128, 8], F32)
    vnT = sb("vnT", [128, 8, 16], F32)
    arena = sb("arena", [128, 3680], F32)
    epsc = sb("epsc", [128, 1], F32)
    pmat = sb("pmat_sb", [128, 12 * 128], BF16)
    pl16 = sb("pl16", [128, 8, 16], BF16)
    fz = sb("fz", [128, 1], F32)
    ps = st.enter_context(nc.psum_tensor("ps", [128, 8, 512], F32))

    R0a = arena[:, 0:512]
    R0b = arena[:, 512:1056]
    R0 = arena[:, 0:1024]
    R1 = arena[:, 1056:2080]
    R2 = arena[:, 2112:3136]
    R3 = arena[:, 3168:3680]
    bR0a, bR0b, bR1 = P.buf("R0a"), P.buf("R0b"), P.buf("R1")
    bR2a, bR2b, bR3 = P.buf("R2a"), P.buf("R2b"), P.buf("R3")
    bR0 = [bR0a, bR0b]
    bR2 = [bR2a, bR2b]

    ident = cstt[:, 0:128]
    maskT = cstt[:, 128:256]
    rc = cstt[:, 256:320]
    sel = cstt[:, 320:352]
    E0 = cstt[:, 352:480]
    bCST = P.buf("cst")
    bONES = P.buf("ones")
    bCOLV = P.buf("colv")
    bSCT = P.buf("scT")
    bPMAT = P.buf("pmat")
    bPL16 = P.buf("pl16")

    X = [[P.buf("x", c, b) for b in range(NB)] for c in range(8)]
    H = [[P.buf("h", c, b) for b in range(NB)] for c in range(8)]
    WK = [[P.buf("wk", c, b) for b in range(NB)] for c in range(8)]
    PSB = [P.buf("ps", k) for k in range(8)]
    SLOT = [P.buf("slot", s) for s in range(NSLOT)]
    MODB = [[P.buf("mod", p, m) for m in range(6)] for p in range(2)]
    SMALLB = [P.buf("small", q) for q in range(3)]
    AMOD = [[P.buf("amod", p, wh) for wh in range(2)] for p in range(2)]
    RSTD = [P.buf("rstd", q) for q in range(2)]
    TMP = [P.buf("tmp", q) for q in range(3)]
    SQ = [P.buf("sq", q) for q in range(3)]
    bHS32, bHLAST, bT16, bT16B = P.buf("hs32"), P.buf("hlast"), P.buf("t16"), P.buf("t16b")
    bSMALL, bWS00, bBS0, bVNT = P.buf("small"), P.buf("ws00"), P.buf("bs0"), P.buf("vnT")

    state = {"bank": 0, "rq": 0, "tq": 0, "sq": 0, "t16b": 0}
    free_slots = list(range(NSLOT))

    held_banks = set()

    def alloc_bank(n=1):
        p = state["bank"]
        for _ in range(32):
            if n == 2 and p % 2 == 1:
                p = (p + 1) % 8
            if p + n > 8:
                p = 0
            if all((p + k_) not in held_banks for k_ in range(n)):
                state["bank"] = (p + n) % 8
                return p
            p = (p + 1) % 8
        raise RuntimeError("no free PSUM bank")

    def nxt(key, n):
        v = state[key]
        state[key] = (v + 1) % n
        return v

    def slot_alloc():
        assert free_slots, "weight ring exhausted"
        return free_slots.pop(0)

    def wrel(s):
        assert s not in free_slots
        free_slots.append(s)

    def wget_cols(W2d, col0, ncols=256):
        s = slot_alloc()
        src = W2d[:, col0:col0 + ncols].rearrange("(k p) n -> p k n", p=128)
        dst = wsl[:, s, 0:8 * ncols].rearrange("p (k n) -> p k n", k=8)
        P.op("gpsimd", lambda e: e.dma_start(out=dst, in_=src), w=[SLOT[s]], dma="w%d" % s)
        return s, dst

    def wget_rows(W2d, row0):
        s = slot_alloc()
        src = W2d[row0:row0 + 256, :].rearrange("(j p) n -> p j n", p=128)
        dst = wsl[:, s, :].rearrange("p (j n) -> p j n", j=2)
        P.op("gpsimd", lambda e: e.dma_start(out=dst, in_=src), w=[SLOT[s]], dma="w%d" % s)
        return s, dst

    def cvcol(name, row):
        o = cvoff[name] + row
        return colv[:, o:o + 1]

    def mm(out, lhsT, rhs, start, stop, r, w):
        P.op("tensor", lambda e: e.matmul(out, lhsT, rhs, start=start, stop=stop), r=r, w=w)

    def tr(out, in_, idn, r, w):
        P.op("tensor", lambda e: e.transpose(out, in_, idn), r=r, w=w)

    def ACT(out, in_, func, r, w, bias=None, scale=None):
        kw = {}
        if bias is not None:
            kw["bias"] = bias
        if scale is not None:
            kw["scale"] = scale
        P.op("scalar", lambda e: e.activation(out=out, in_=in_, func=func, **kw), r=r, w=w)

    def ACOPY(out, in_, r, w):
        P.op("scalar", lambda e: e.copy(out=out, in_=in_), r=r, w=w)

    def VCOPY(out, in_, r, w):
        P.op("vector", lambda e: e.tensor_copy(out=out, in_=in_), r=r, w=w)

    def TT(out, in0, in1, op, r, w):
        P.op("vector", lambda e: e.tensor_tensor(out=out, in0=in0, in1=in1, op=op), r=r, w=w)

    def TS(out, in0, s1, s2, op0, op1, r, w):
        if op1 is None:
            P.op("vector", lambda e: e.tensor_scalar(out=out, in0=in0, scalar1=s1, scalar2=None, op0=op0), r=r, w=w)
        else:
            P.op("vector", lambda e: e.tensor_scalar(out=out, in0=in0, scalar1=s1, scalar2=s2, op0=op0, op1=op1),
                 r=r, w=w)

    def STT(out, in0, scalar, in1, op0, op1, r, w):
        P.op("vector", lambda e: e.scalar_tensor_tensor(out=out, in0=in0, scalar=scalar, in1=in1, op0=op0, op1=op1),
             r=r, w=w)

    def DMA(eng, out, in_, sem, r=(), w=()):
        P.op(eng, lambda e: e.dma_start(out=out, in_=in_), r=r, w=w, dma=sem)

    def MEMSET(ap, val, w):
        P.op("vector", lambda e: e.memset(ap, val), w=w)

    ZTH = [[P.buf("zt", zi, hf) for hf in range(2)] for zi in range(3)]
    ALIAS_R23 = [bR2a, bR2b, bR3] + [b_ for pr in ZTH for b_ in pr]

    def FENCE(bufs):
        P.op("vector", lambda e: e.memset(fz[:, :], 0.0), w=bufs)

    def rsqrt_act(out, in_, npart, r, w):
        ACT(out, in_, AF.Ln, r=list(r) + [bONES], w=w, bias=epsc[0:npart, :], scale=1.0)
        ACT(out, out, AF.Exp, r=w, w=w, scale=-0.5)

    def pipeline(items, order, per_step=None):
        n = len(order)
        max_off = 0
        for it in items:
            max_off = max(max_off, it[1])
            if len(it) > 2:
                for _, co_off in it[2]:
                    max_off = max(max_off, co_off)
        for step in range(n + max_off):
            if per_step is not None:
                per_step()
            for it in items:
                fn, off = it[0], it[1]
                jj = step - off
                has_fn = 0 <= jj < n
                if len(it) > 2:
                    act = [(cf, order[step - co]) for cf, co in it[2] if 0 <= step - co < n]

                    def co(c, act=act):
                        for cf, bb in act:
                            cf(bb, c)
                    if has_fn:
                        fn(order[jj], co if act else None)
                    elif act:
                        for c in range(-1, 8):
                            co(c)
                elif has_fn:
                    fn(order[jj])

    ORDER = [SB] + list(range(NPB))

    DMA("sync", cstt[:, :], cst[:, :], "init", w=[bCST])
    MEMSET(ones_m[:, :], 1.0 / 1024.0, [bONES])
    MEMSET(ones_1[:, :], 1.0, [bONES])
    MEMSET(epsc[:, :], EPS, [bONES])
    MEMSET(hT[:, :, 0:16], 0.0, [H[c][0] for c in range(8)])


    cvst = R3
    srcs = {
        "b_ada": b_ada.rearrange("l (r p) -> (l r) p", p=128),
        "norm_mix": norm_mix.rearrange("l (r p) -> (l r) p", p=128),
        "norm_ffn": norm_ffn.rearrange("l (r p) -> (l r) p", p=128),
        "norm_final": norm_final.rearrange("(r p) -> r p", p=128),
        "pool_scale": pool_scale.rearrange("l (r p) -> (l r) p", p=128),
        "gmlp_ln_g": gmlp_ln_g.rearrange("l (r p) -> (l r) p", p=128),
        "gmlp_ln_b": gmlp_ln_b.rearrange("l (r p) -> (l r) p", p=128),
    }
    assert NCVT <= 4
    for name, nrows in segs:
        o = cvoff[name]
        done = 0
        while done < nrows:
            t = (o + done) // 128
            p0 = (o + done) % 128
            n = min(nrows - done, 128 - p0)
            DMA("sync", cvst[p0:p0 + n, t * 128:(t + 1) * 128], srcs[name][done:done + n, :], "init", w=[bR3])
            done += n
    for t in range(NCVT):
        nrow = min(128, NCV - t * 128)
        bk = alloc_bank()
        tr(ps[:, bk, 0:nrow], cvst[0:nrow, t * 128:(t + 1) * 128], ident[0:nrow, 0:nrow],
           r=[bR3, bCST], w=[PSB[bk]])
        VCOPY(colv[:, t * 128:t * 128 + nrow], ps[:, bk, 0:nrow], r=[PSB[bk]], w=[bCOLV])

    DMA("sync", R2[0:1 + NS, :], cc[:, :], "init", w=bR2)
    bk = alloc_bank()
    for c in range(8):
        tr(ps[:, bk, c * 17:(c + 1) * 17], R2[0:17, c * 128:(c + 1) * 128], ident[0:17, 0:17],
           r=bR2 + [bCST], w=[PSB[bk]])
    ACT(scT[:, :, :], ps[:, bk, 0:136].rearrange("p (c n) -> p c n", c=8), AF.Silu, r=[PSB[bk]], w=[bSCT])

    ada_state = {"layer": 0, "next": 0}

    def ada_tile(i, t):
        par = i % 2
        m = t // 4
        s, wv = wget_cols(w_ada[i], t * 256)
        for oc in range(2):
            ch = 2 * t + oc
            bk = alloc_bank()
            for k in range(8):
                mm(ps[:, bk, 0:17], wv[:, k, oc * 128:(oc + 1) * 128], scT[:, k, :], k == 0, k == 7,
                   r=[SLOT[s], bSCT], w=[PSB[bk]])
            TS(mod[:, par, ch, :], ps[:, bk, 0:17], cvcol("b_ada", i * 48 + ch), None, ALU.add, None,
               r=[PSB[bk], bCOLV], w=[MODB[par][m]])
        wrel(s)
        if t % 4 == 3 and m in (1, 4):
            wh = 0 if m == 1 else 1
            nname = "norm_mix" if wh == 0 else "norm_ffn"
            for c in range(8):
                TS(Amod[:, par, wh, c, :], mod[:, par, m * 8 + c, :], 1.0, cvcol(nname, i * 8 + c),
                   ALU.add, ALU.mult, r=[MODB[par][m], bCOLV], w=[AMOD[par][wh]])

    def ada_pop(n=1):
        for _ in range(n):
            i, t = ada_state["layer"], ada_state["next"]
            if i >= DEPTH:
                return
            ada_tile(i, t)
            if t == 23:
                ada_state["layer"], ada_state["next"] = i + 1, 0
            else:
                ada_state["next"] = t + 1

    def ada_need(i, m):
        while (ada_state["layer"], ada_state["next"]) <= (i, 4 * m + 3) and ada_state["layer"] < DEPTH:
            ada_pop()

    def ada_allow(i):
        return ada_state["layer"] <= i

    stg = [R0, R1]
    bstg = [bR0, [bR1]]
    if TOK >= 2048:
        for c_ in range(4):
            stg.append(wk[:, c_, 0:2048].bitcast(F32))
            bstg.append([WK[c_][b_] for b_ in range(NPB)])
    NSTG = len(stg)
    xl = {"n": 0}

    def xload_block(b, co=None):
        if b == SB:
            q = xl["n"] % NSTG
            xl["n"] += 1
            DMA("sync", stg[q][0:NS, :], xs[:, :], "xin%d" % q, w=bstg[q])
            bk = alloc_bank()
            for c in range(8):
                tr(ps[:, bk, c * 16:(c + 1) * 16], stg[q][0:NS, c * 128:(c + 1) * 128], ident[0:NS, 0:NS],
                   r=bstg[q] + [bCST], w=[PSB[bk]])
            VCOPY(xT[:, :, SEQ:SEQ + NS], ps[:, bk, 0:128].rearrange("p (c n) -> p c n", c=8),
                  r=[PSB[bk]], w=[X[c][SB] for c in range(8)])
            if co is not None:
                for c in range(-1, 8):
                    co(c)
            return
        if co is not None:
            co(-1)
        for ti, tt in enumerate(range(4 * b, 4 * b + 4)):
            q = xl["n"] % NSTG
            xl["n"] += 1
            DMA("sync", stg[q], xp[tt * 128:(tt + 1) * 128, :], "xin%d" % q, w=bstg[q])
            bk = alloc_bank(2)
            for c in range(8):
                tr(ps[:, bk + c // 4, (c % 4) * 128:(c % 4 + 1) * 128], stg[q][:, c * 128:(c + 1) * 128],
                   ident, r=bstg[q] + [bCST], w=[PSB[bk + c // 4]])
            for hh in range(2):
                o_ap = xT[:, 4 * hh:4 * hh + 4, tt * 128:(tt + 1) * 128]
                i_ap = ps[:, bk + hh, :].rearrange("p (c n) -> p c n", c=4)
                wb = [X[c][b] for c in range(4 * hh, 4 * hh + 4)]
                if hh == 0:
                    ACOPY(o_ap, i_ap, r=[PSB[bk + hh]], w=wb)
                else:
                    VCOPY(o_ap, i_ap, r=[PSB[bk + hh]], w=wb)
            if co is not None:
                co(2 * ti)
                co(2 * ti + 1)

    nst = {}

    def norm_sq(key, b, c):
        col0, n = blocks[b]
        q = nxt("sq", 3)
        ACT(sq[:, q, 0:n], xT[:, c, col0:col0 + n], AF.Square, r=[X[c][b]], w=[SQ[q]])
        nst[(key, b, "q", c)] = q

    def norm_a(key, b, c):
        col0, n = blocks[b]
        if c == -1:
            bk = alloc_bank()
            nst[(key, b)] = bk
            held_banks.add(bk)
            norm_sq(key, b, 0)
            return
        bk = nst[(key, b)]
        q = nst.pop((key, b, "q", c))
        mm(ps[:, bk, 0:n], ones_m[:, :], sq[:, q, 0:n], c == 0, c == 7, r=[SQ[q], bONES], w=[PSB[bk]])
        if c < 7:
            norm_sq(key, b, c + 1)

    def norm_rstd(key, b):
        col0, n = blocks[b]
        bk = nst.pop((key, b))
        held_banks.discard(bk)
        rq = nxt("rq", 2)
        ACT(rstd[:, rq, 0:n], ps[:, bk, 0:n], AF.Ln, r=[PSB[bk], bONES], w=[RSTD[rq]], bias=epsc[:, :], scale=1.0)
        ACT(rstd[:, rq, 0:n], rstd[:, rq, 0:n], AF.Exp, r=[RSTD[rq]], w=[RSTD[rq]], scale=-0.5)
        return rq

    def norm_b(i, wh, pool_layer, b, c):
        par = i % 2
        col0, n = blocks[b]
        if c == -1:
            ada_need(i, 3 * wh + 1)
            nst[("rq", i, wh, b)] = norm_rstd((i, wh), b)
            return
        rq = nst[("rq", i, wh, b)]
        if c == 7:
            nst.pop(("rq", i, wh, b))
        MSH = MODB[par][3 * wh]
        tq = nxt("tq", 3)
        A = Amod[:, par, wh, c, :]
        sh = mod[:, par, (3 * wh) * 8 + c, :]
        if b != SB:
            STT(tmp[:, tq, 0:n], xT[:, c, col0:col0 + n], A[:, 0:1], rstd[:, rq, 0:n], ALU.mult, ALU.mult,
                r=[X[c][b], AMOD[par][wh], RSTD[rq]], w=[TMP[tq]])
            ACT(hT[:, c, 16 + col0:16 + col0 + n], tmp[:, tq, 0:n], AF.Identity,
                r=[TMP[tq], MSH], w=[H[c][b]], bias=sh[:, 0:1], scale=1.0)
            if pool_layer and wh == 0 and b == NPB - 1:
                ACT(hlast[:, c, :], tmp[:, tq, n - 16:n], AF.Identity,
                    r=[TMP[tq], MSH], w=[bHLAST], bias=sh[:, 0:1], scale=1.0)
        else:
            TT(tmp[:, tq, 0:NS], xT[:, c, col0:col0 + NS], rstd[:, rq, 0:NS], ALU.mult,
               r=[X[c][b], RSTD[rq]], w=[TMP[tq]])
            TT(tmp[:, tq, 0:NS], tmp[:, tq, 0:NS], A[:, 1:17], ALU.mult,
               r=[TMP[tq], AMOD[par][wh]], w=[TMP[tq]])
            TT(hs32[:, c, :], tmp[:, tq, 0:NS], sh[:, 1:17], ALU.add,
               r=[TMP[tq], MSH], w=[bHS32])
            ACOPY(hT[:, c, 16 + SEQ:16 + SEQ + NS], hs32[:, c, :], r=[bHS32], w=[H[c][b]])

    def x_update(bk, oc, b, par, m):
        col0, n = blocks[b]
        g = mod[:, par, m * 8 + oc, :]
        if b != SB:
            STT(xT[:, oc, col0:col0 + n], ps[:, bk, 0:n], g[:, 0:1], xT[:, oc, col0:col0 + n], ALU.mult, ALU.add,
                r=[PSB[bk], MODB[par][m], X[oc][b]], w=[X[oc][b]])
        else:
            tb = nxt("t16b", 2)
            TT(t16b[:, tb, :], ps[:, bk, 0:NS], g[:, 1:17], ALU.mult, r=[PSB[bk], MODB[par][m]], w=[bT16B])
            TT(xT[:, oc, col0:col0 + NS], t16b[:, tb, :], xT[:, oc, col0:col0 + NS], ALU.add,
               r=[bT16B, X[oc][b]], w=[X[oc][b]])

    def make_outproj(i, W2d, src, SRC, m):
        par = i % 2
        st_ = {"tiles": None, "done": 0}

        def prefetch():
            if st_["tiles"] is None:
                ada_need(i, m)
                st_["tiles"] = [wget_cols(W2d, t * 256) for t in range(4)]

        def stage(b, co=None):
            prefetch()
            col0, n = blocks[b]
            if co is not None:
                co(-1)
            for oc in range(8):
                s, wv = st_["tiles"][oc // 2]
                oc2 = oc % 2
                bk = alloc_bank()
                for k in range(8):
                    mm(ps[:, bk, 0:n], wv[:, k, oc2 * 128:(oc2 + 1) * 128], src(k, col0, n), k == 0, k == 7,
                       r=[SLOT[s], SRC[k][b]], w=[PSB[bk]])
                if co is not None:
                    co(oc)
                x_update(bk, oc, b, par, m)
            st_["done"] += 1
            if st_["done"] == NB:
                for s, _ in st_["tiles"]:
                    wrel(s)
        stage.prefetch = prefetch
        return stage

    def tok_major_out(srcT, bsrc, dst_ap_fn, stage, bstage, sem):
        bk = alloc_bank(2)
        for c in range(8):
            tr(ps[0:16, bk + c // 4, (c % 4) * 128:(c % 4 + 1) * 128], srcT[:, c, :], ident,
               r=[bsrc, bCST], w=[PSB[bk + c // 4]])
        VCOPY(stage[0:16, :].rearrange("p (a n) -> p a n", a=2), ps[0:16, bk:bk + 2, :],
              r=[PSB[bk], PSB[bk + 1]], w=bstage)
        o_ap, i_ap = dst_ap_fn(stage)
        DMA("sync", o_ap, i_ap, sem, r=bstage)

    def make_pool(i):
        par = i % 2
        j = i // 2
        zt = [R2[:, 0:512].bitcast(BF16), R2[:, 512:1024].bitcast(BF16), R3.bitcast(BF16)]
        bzt = ZTH
        FENCE(ALIAS_R23)
        gst = {"slot": None, "gv": None, "done": 0, "zi": 0, "pend": None, "prev": None}

        def get_w():
            if gst["slot"] is None:
                s = slot_alloc()
                gv = wsl[:, s, :].rearrange("p (g k n) -> p g k n", g=4, k=2)
                for g in range(4):
                    DMA("gpsimd", gv[:, g, :, :], pool_w_grp[j, g].rearrange("(k p) n -> p k n", p=128),
                        "w%d" % s, w=[SLOT[s]])
                DMA("sync", R1, pool_scale[j].partition_broadcast(128), "pscale", w=[bR1])
                for k in range(2):
                    TT(gv[:, :, k, :], gv[:, :, k, :], R1.rearrange("p (g n) -> p g n", g=4), ALU.mult,
                       r=[SLOT[s], bR1], w=[SLOT[s]])
                gst["slot"], gst["gv"] = s, gv
            return gst["slot"], gst["gv"]

        def zA(tt):
            s, gv = get_w()
            b = tt // 4
            zi = gst["zi"] % 3
            gst["zi"] += 1
            bz = alloc_bank(2)
            for g in range(4):
                for k in range(2):
                    mm(ps[:, bz + g // 2, (g % 2) * 256:(g % 2 + 1) * 256],
                       hT[:, 2 * g + k, 16 + tt * 128:16 + (tt + 1) * 128], gv[:, g, k, :], k == 0, k == 1,
                       r=[SLOT[s], H[2 * g + k][b]], w=[PSB[bz + g // 2]])
            ACOPY(zt[zi][:, 0:512], ps[:, bz, :], r=[PSB[bz]], w=[bzt[zi][0]])
            VCOPY(zt[zi][:, 512:1024], ps[:, bz + 1, :], r=[PSB[bz + 1]], w=[bzt[zi][1]])
            return zi

        def zB(tt, zi, zprev):
            b = tt // 4
            for quad in range(2):
                bp = alloc_bank()
                for ci in range(4):
                    c = 4 * quad + ci
                    wi = c // 2
                    kind = 2 if tt == 0 else 0
                    pa = pmat[:, (kind * 4 + wi) * 128:(kind * 4 + wi + 1) * 128]
                    mm(ps[:, bp, ci * 128:(ci + 1) * 128], zt[zi][:, c * 128:(c + 1) * 128], pa, True, tt == 0,
                       r=[bzt[zi][quad], bPMAT], w=[PSB[bp]])
                    if tt > 0:
                        pb = pmat[:, (4 + wi) * 128:(4 + wi) * 128 + 16]
                        mm(ps[:, bp, ci * 128:ci * 128 + 16], zt[zprev][:, c * 128:(c + 1) * 128], pb, False, True,
                           r=[bzt[zprev][quad], bPMAT], w=[PSB[bp]])
                VCOPY(wk[:, 4 * quad:4 * quad + 4, tt * 128:(tt + 1) * 128],
                      ps[:, bp, :].rearrange("p (c n) -> p c n", c=4),
                      r=[PSB[bp]], w=[WK[c][b] for c in range(4 * quad, 4 * quad + 4)])

        HPRE = (TOK >= 2048)
        if HPRE:
            hbuf = [wk[:, 6 + t_, 0:2048].bitcast(F32) for t_ in range(2)]
            hb = [[WK[6 + t_][b_] for b_ in range(NB)] for t_ in range(2)]
            for t_ in range(2):
                DMA("sync", hbuf[t_][0:120, :], sp[j, 8 * t_:8 * t_ + 8].rearrange("b t d -> (b t) d"),
                    "hist%d" % t_, w=hb[t_])

        def zpool(b):
            if b == SB:
                s, gv = get_w()
                bk = alloc_bank()
                for tile in range(2):
                    if HPRE:
                        hsrc, hbufs = hbuf[tile], hb[tile]
                    else:
                        DMA("sync", R1[0:120, :], sp[j, 8 * tile:8 * tile + 8].rearrange("b t d -> (b t) d"), "hist",
                            w=[bR1])
                        hsrc, hbufs = R1, [bR1]
                    for c in range(8):
                        wi = c // 2
                        mm(ps[:, bk, c * 16 + 8 * tile:c * 16 + 8 * tile + 8], hsrc[0:120, c * 128:(c + 1) * 128],
                           sel[0:120, wi * 8:(wi + 1) * 8], True, True, r=hbufs + [bCST], w=[PSB[bk]])
                for c in range(8):
                    w = WINDOWS[c // 2]
                    TS(t16[:, c, :], hs32[:, c, :], (1.0 / w - 1.0), None, ALU.mult, None, r=[bHS32], w=[bT16])
                    STT(pl16[:, c, :], ps[:, bk, c * 16:(c + 1) * 16], 1.0 / w, t16[:, c, :],
                        ALU.mult, ALU.add, r=[PSB[bk], bT16], w=[bPL16])
                tok_major_out(hs32, bHS32, lambda stage: (nps[j, :, PB - 1, :], stage[0:16, :]), R1, [bR1], "out_ns")
                bk = alloc_bank()
                for g in range(4):
                    for oc in range(2):
                        ch = 2 * g + oc
                        for k in range(2):
                            mm(ps[:, bk, ch * 16:(ch + 1) * 16], gv[:, g, k, oc * 128:(oc + 1) * 128],
                               pl16[:, 2 * g + k, :], k == 0, k == 1, r=[SLOT[s], bPL16], w=[PSB[bk]])
                VCOPY(wk[:, :, SEQ:SEQ + NS], ps[:, bk, 0:128].rearrange("p (c n) -> p c n", c=8),
                      r=[PSB[bk]], w=[WK[c][SB] for c in range(8)])
            else:
                for tt in range(4 * b, 4 * b + 4):
                    zi = zA(tt)
                    if gst["pend"] is not None:
                        zB(*gst["pend"])
                    gst["pend"] = (tt, zi, gst["prev"])
                    gst["prev"] = zi
                if b == NPB - 1:
                    zB(*gst["pend"])
                    gst["pend"] = None
                    tok_major_out(hlast, bHLAST, lambda stage: (npp[j, :, :], stage[1:16, :]), R1, [bR1], "out_np")
            gst["done"] += 1
            if gst["done"] == NB:
                wrel(gst["slot"])

        outp = make_outproj(i, pool_w_out[j], lambda k, col0, n: wk[:, k, col0:col0 + n], WK, 2)
        zpool_inner = zpool

        def zpool_pf(b):
            zpool_inner(b)
            if b == SB:
                outp.prefetch()
        return zpool_pf, outp

    def gmlp_prep_parts(i):
        j = i // 2
        wss = R0.rearrange("p (g s) -> p g s", g=8)
        Cb = R1
        WsT = R3.bitcast(BF16).rearrange("p (g t) -> p g t", g=8)

        def part0():
            FENCE(ALIAS_R23)
            DMA("sync", wss, gmlp_w_s[j].rearrange("g t s -> t g s"), "wss", w=bR0)
            DMA("sync", Cb, gmlp_b_s[j].rearrange("g t -> (g t)").partition_broadcast(128), "cb", w=[bR1])

        def part1():
            VCOPY(bs0[:, :], R1[:, 0:1024:128], r=[bR1], w=[bBS0])
            bk = alloc_bank()
            mm(ps[:, bk, 0:8], E0, R0[:, 0:1024:128], True, True, r=bR0 + [bCST], w=[PSB[bk]])
            VCOPY(ws00[:, :], ps[:, bk, 0:8], r=[PSB[bk]], w=[bWS00])
            for half in range(2):
                bk = alloc_bank()
                for gl in range(4):
                    g = 4 * half + gl
                    tr(ps[:, bk, gl * 128:(gl + 1) * 128], wss[:, g, :], ident, r=bR0 + [bCST], w=[PSB[bk]])
                for gl in range(4):
                    g = 4 * half + gl
                    TT(WsT[:, g, :], ps[:, bk, gl * 128:(gl + 1) * 128], maskT, ALU.mult,
                       r=[PSB[bk], bCST], w=[bR3])

        def part2():
            for half in range(2):
                bk = alloc_bank()
                for gl in range(4):
                    g = 4 * half + gl
                    mm(ps[:, bk, gl * 128:(gl + 1) * 128], ones_1[:, :], WsT[:, g, :], True, True,
                       r=[bONES, bR3], w=[PSB[bk]])
                for gl in range(4):
                    g = 4 * half + gl
                    STT(Cb[:, g * 128:(g + 1) * 128], ps[:, bk, gl * 128:(gl + 1) * 128],
                        cvcol("gmlp_ln_b", j * 8 + g), Cb[:, g * 128:(g + 1) * 128], ALU.mult, ALU.add,
                        r=[PSB[bk], bCOLV, bR1], w=[bR1])
        return [part0, part1, part2]

    def ln_stats(bk, npart, vq):
        o = vq * 16
        bsm = SMALLB[vq]
        P.op("vector", lambda e: e.bn_stats(out=small[0:npart, o:o + 6], in_=ps[0:npart, bk, :]),
             r=[PSB[bk]], w=[bsm])
        P.op("vector", lambda e: e.bn_stats(out=small[0:npart, o + 6:o + 12], in_=ps[0:npart, bk + 1, :]),
             r=[PSB[bk + 1]], w=[bsm])
        P.op("vector", lambda e: e.bn_aggr(out=small[0:npart, o + 12:o + 14], in_=small[0:npart, o:o + 12]),
             r=[bsm], w=[bsm])
        rsqrt_act(small[0:npart, o + 13:o + 14], small[0:npart, o + 13:o + 14], npart, r=[bsm], w=[bsm])
        TS(small[0:npart, o + 14:o + 15], small[0:npart, o + 12:o + 13], small[0:npart, o + 13:o + 14], -1.0,
           ALU.mult, ALU.mult, r=[bsm], w=[bsm])
        return small[0:npart, o + 13:o + 14], small[0:npart, o + 14:o + 15]

    def make_gmlp(i):
        par = i % 2
        j = i // 2
        Cb = R1
        WsT = R3.bitcast(BF16).rearrange("p (g t) -> p g t", g=8)
        vt = [R2[:, 0:512].bitcast(BF16), R2[:, 512:1024].bitcast(BF16), arena[:, 0:512].bitcast(BF16)]
        bvt = [bR2a, bR2b, bR0a]
        gs = {"vs": None, "pend": [], "cnt": 2, "samp": False}

        def vA(tt):
            if gs["vs"] is None:
                gs["vs"] = [wget_cols(gmlp_w_in[j], 1024 + t * 256) for t in range(4)]
            samp = tt == NTT
            npart = NS if samp else 128
            hc0 = 16 + tt * 128
            b = SB if samp else tt // 4
            bk = alloc_bank(2)
            for t in range(4):
                s, wv = gs["vs"][t]
                for k in range(8):
                    mm(ps[0:npart, bk + t // 2, (t % 2) * 256:(t % 2 + 1) * 256], hT[:, k, hc0:hc0 + npart],
                       wv[:, k, :], k == 0, k == 7, r=[SLOT[s], H[k][b]], w=[PSB[bk + t // 2]])
            if not samp:
                vq = gs["cnt"] % 3
                gs["cnt"] += 1
                rs_ap, nb_ap = ln_stats(bk, npart, vq)
                ACT(vt[vq].rearrange("p (a n) -> p a n", a=2), ps[:, bk:bk + 2, :], AF.Identity,
                    r=[PSB[bk], PSB[bk + 1], SMALLB[vq]], w=[bvt[vq]], bias=nb_ap, scale=rs_ap)
            else:
                vq = 0
                rs_ap, nb_ap = ln_stats(bk, npart, vq)
                zs = R2[0:NS, :]
                ACT(zs.rearrange("p (a n) -> p a n", a=2), ps[0:NS, bk:bk + 2, :], AF.Identity,
                    r=[PSB[bk], PSB[bk + 1], SMALLB[vq]], w=bR2, bias=nb_ap, scale=rs_ap)
            return vq

        def vB(tt, vq):
            samp = tt == NTT
            if not samp:
                b = tt // 4
                for half in range(2):
                    bs_ = alloc_bank()
                    for gl in range(4):
                        g = 4 * half + gl
                        mm(ps[:, bs_, gl * 128:(gl + 1) * 128], vt[vq][:, g * 128:(g + 1) * 128],
                           WsT[:, g, :], True, True, r=[bvt[vq], bR3], w=[PSB[bs_]])
                    for gl in range(4):
                        g = 4 * half + gl
                        STT(wk[:, g, tt * 128:(tt + 1) * 128], ps[:, bs_, gl * 128:(gl + 1) * 128],
                            cvcol("gmlp_ln_g", j * 8 + g), Cb[:, g * 128:(g + 1) * 128], ALU.mult, ALU.add,
                            r=[PSB[bs_], bCOLV, bR1], w=[WK[g][b]])
            else:
                bz = alloc_bank()
                for g in range(8):
                    tr(ps[:, bz, g * 16:(g + 1) * 16], R2[0:NS, g * 128:(g + 1) * 128], ident[0:NS, 0:NS],
                       r=bR2 + [bCST], w=[PSB[bz]])
                for g in range(8):
                    TS(vnT[:, g, :], ps[:, bz, g * 16:(g + 1) * 16], cvcol("gmlp_ln_g", j * 8 + g),
                       cvcol("gmlp_ln_b", j * 8 + g), ALU.mult, ALU.add, r=[PSB[bz], bCOLV], w=[bVNT])
                tok_major_out(vnT, bVNT, lambda stage: (ncv[j, :, :], stage[0:16, :]), R2, bR2, "out_v")
                for g in range(8):
                    TS(wk[:, g, SEQ:SEQ + NS], vnT[:, g, :], ws00[:, g:g + 1], bs0[:, g:g + 1], ALU.mult, ALU.add,
                       r=[bVNT, bWS00, bBS0], w=[WK[g][SB]])

        def v_stage(b):
            if b == SB:
                vA(NTT)
                gs["samp"] = True
                return
            for tt in range(4 * b, 4 * b + 4):
                vq = vA(tt)
                if gs["samp"]:
                    vB(NTT, 0)
                    gs["samp"] = False
                gs["pend"].append((tt, vq))
                if len(gs["pend"]) > 2:
                    vB(*gs["pend"].pop(0))

        def v_flush():
            assert not gs["samp"]
            while gs["pend"]:
                vB(*gs["pend"].pop(0))
            for s, _ in gs["vs"]:
                wrel(s)

        def u_phase():
            for t in range(4):
                s, wv = wget_cols(gmlp_w_in[j], t * 256)
                for oc2 in range(2):
                    oc = 2 * t + oc2
                    for b in ORDER:
                        col0, n = blocks[b]
                        bk = alloc_bank()
                        for k in range(8):
                            mm(ps[:, bk, 0:n], wv[:, k, oc2 * 128:(oc2 + 1) * 128],
                               hT[:, k, 16 + col0:16 + col0 + n], k == 0, k == 7,
                               r=[SLOT[s], H[k][b]], w=[PSB[bk]])
                        TT(wk[:, oc, col0:col0 + n], ps[:, bk, 0:n], wk[:, oc, col0:col0 + n], ALU.mult,
                           r=[PSB[bk], WK[oc][b]], w=[WK[oc][b]])
                wrel(s)
                if ada_allow(i + 1):
                    ada_pop(1)

        outp = make_outproj(i, gmlp_w_out[j], lambda k, col0, n: wk[:, k, col0:col0 + n], WK, 2)
        return v_stage, v_flush, u_phase, outp

    def make_ffn(i):
        par = i % 2
        groups = [[0, 1, 2, 3], [4, 5, 6, 7], [8, 9, 10]]

        def gate_up(grp):
            for ti, t in enumerate(grp):
                sg, gvw = wget_cols(ffn_w_gate[i], t * 256)
                su, uvw = wget_cols(ffn_w_up[i], t * 256)
                for fc in range(2):
                    wc = 2 * ti + fc
                    for b in ORDER:
                        col0, n = blocks[b]
                        bg = alloc_bank()
                        for k in range(8):
                            mm(ps[:, bg, 0:n], gvw[:, k, fc * 128:(fc + 1) * 128], hT[:, k, 16 + col0:16 + col0 + n],
                               k == 0, k == 7, r=[SLOT[sg], H[k][b]], w=[PSB[bg]])
                        bu = alloc_bank()
                        for k in range(8):
                            mm(ps[:, bu, 0:n], uvw[:, k, fc * 128:(fc + 1) * 128], hT[:, k, 16 + col0:16 + col0 + n],
                               k == 0, k == 7, r=[SLOT[su], H[k][b]], w=[PSB[bu]])
                        q = nxt("sq", 3)
                        ACT(sq[:, q, 0:n], ps[:, bg, 0:n], AF.Silu, r=[PSB[bg]], w=[SQ[q]])
                        TT(wk[:, wc, col0:col0 + n], ps[:, bu, 0:n], sq[:, q, 0:n], ALU.mult,
                           r=[PSB[bu], SQ[q]], w=[WK[wc][b]])
                wrel(sg)
                wrel(su)
                if ada_allow(i + 1):
                    ada_pop(2)

        def make_down(grp):
            st_ = {"sd": None, "done": 0}
            nk = 2 * len(grp)

            def stage(b, co=None):
                if st_["sd"] is None:
                    ada_need(i, 5)
                    st_["sd"] = [wget_rows(ffn_w_down[i], t * 256) for t in grp]
                col0, n = blocks[b]
                if co is not None:
                    co(-1)
                for oc in range(8):
                    bk = alloc_bank()
                    for kk in range(nk):
                        s, dv = st_["sd"][kk // 2]
                        mm(ps[:, bk, 0:n], dv[:, kk % 2, oc * 128:(oc + 1) * 128], wk[:, kk, col0:col0 + n],
                           kk == 0, kk == nk - 1, r=[SLOT[s], WK[kk][b]], w=[PSB[bk]])
                    if co is not None:
                        co(oc)
                    x_update(bk, oc, b, par, 5)
                st_["done"] += 1
                if st_["done"] == NB:
                    for s, _ in st_["sd"]:
                        wrel(s)
            return stage

        def front(hooks=()):
            hooks = list(hooks)
            if hooks:
                hooks.pop(0)()
            for gi, grp in enumerate(groups):
                gate_up(grp)
                if hooks:
                    hooks.pop(0)()
                if gi < len(groups) - 1:
                    dn = make_down(grp)
                    for b in ORDER:
                        dn(b)
                    if ada_allow(i + 1):
                        ada_pop(1)
            while hooks:
                hooks.pop(0)()
            return make_down(groups[-1])
        return front

    fin = {"ot": 0}
    ostg = [R0, R1]
    bostg = [bR0, [bR1]]

    gfb = tmp[:, 0:2, :].rearrange("p a n -> p (a n)")

    def final_prep():
        DMA("sync", gfb, norm_final.partition_broadcast(128), "gfb", w=[TMP[0], TMP[1]])

    def final_b(b):
        col0, n = blocks[b]
        nsub = 1 if b == SB else n // 128
        npart = NS if b == SB else 128
        for s_ in range(nsub):
            c0 = col0 + s_ * 128
            bb = alloc_bank(2)
            for c in range(8):
                tr(ps[0:npart, bb + c // 4, (c % 4) * 128:(c % 4 + 1) * 128], xT[:, c, c0:c0 + npart], ident,
                   r=[X[c][b], bCST], w=[PSB[bb + c // 4]])
            vq = fin["ot"] % 3
            o = vq * 16
            bsm = SMALLB[vq]
            P.op("vector", lambda e, o=o, bb=bb: e.bn_stats(out=small[0:npart, o:o + 6], in_=ps[0:npart, bb, :]),
                 r=[PSB[bb]], w=[bsm])
            P.op("vector", lambda e, o=o, bb=bb: e.bn_stats(out=small[0:npart, o + 6:o + 12],
                                                             in_=ps[0:npart, bb + 1, :]),
                 r=[PSB[bb + 1]], w=[bsm])
            P.op("vector", lambda e, o=o: e.bn_aggr(out=small[0:npart, o + 12:o + 14], in_=small[0:npart, o:o + 12]),
                 r=[bsm], w=[bsm])
            STT(small[0:npart, o + 14:o + 15], small[0:npart, o + 12:o + 13], small[0:npart, o + 12:o + 13],
                small[0:npart, o + 13:o + 14], ALU.mult, ALU.add, r=[bsm], w=[bsm])
            rsqrt_act(small[0:npart, o + 14:o + 15], small[0:npart, o + 14:o + 15], npart, r=[bsm], w=[bsm])
            q = fin["ot"] % 2
            fin["ot"] += 1
            ACT(ostg[q][0:npart, :].rearrange("p (a n) -> p a n", a=2), ps[0:npart, bb:bb + 2, :], AF.Identity,
                r=[PSB[bb], PSB[bb + 1], bsm], w=bostg[q], scale=small[0:npart, o + 14:o + 15])
            P.op("gpsimd", lambda e, q=q: e.tensor_tensor(out=ostg[q][0:npart, :], in0=ostg[q][0:npart, :],
                                                            in1=gfb[0:npart, :], op=ALU.mult),
                 r=bostg[q] + [TMP[0], TMP[1]], w=bostg[q])
            if b == SB:
                DMA("sync", ys[:, :], ostg[q][0:NS, :], "out_y%d" % q, r=bostg[q])
            else:
                DMA("sync", yp[c0:c0 + 128, :], ostg[q], "out_y%d" % q, r=bostg[q])

    ada_need(0, 1)
    DMA("gpsimd", pmat[:, :], pmat_d[:, :], "pm", w=[bPMAT])
    prev_stage = xload_block
    for i in range(DEPTH):
        pool_layer = (i % 2 == 0)
        n1a = (lambda b, c, i=i: norm_a((i, 0), b, c))
        n1b = (lambda b, c, i=i, pl=pool_layer: norm_b(i, 0, pl, b, c))
        n2a = (lambda b, c, i=i: norm_a((i, 1), b, c))
        n2b = (lambda b, c, i=i, pl=pool_layer: norm_b(i, 1, pl, b, c))

        stepc = {"n": 0}

        def step_pop(i=i, stepc=stepc):
            if ada_allow(i):
                if i == 0:
                    ada_pop(1 if stepc["n"] < 5 else 3)
                else:
                    ada_pop(2)
            stepc["n"] += 1

        if pool_layer:
            zpool, outp = make_pool(i)
            pipeline([(prev_stage, 0, [(n1a, 1), (n1b, 2)]), (zpool, 3), (outp, 4, [(n2a, 5), (n2b, 6)])],
                     ORDER, per_step=step_pop)
        else:
            v_stage, v_flush, u_phase, outp = make_gmlp(i)
            pipeline([(prev_stage, 0, [(n1a, 1), (n1b, 2)]), (v_stage, 3)], ORDER, per_step=step_pop)
            v_flush()
            outp.prefetch()
            u_phase()
            pipeline([(outp, 0, [(n2a, 1), (n2b, 2)])], ORDER, per_step=step_pop)
        ada_need(i, 5)
        if i == 0:
            for j in range(NPOOL):
                DMA("sync", nps[j, :, 0:PB - 1, :], sp[j, :, 1:PB, :], "out_hist")
        nxt_prep = gmlp_prep_parts(i + 1) if (i + 1 < DEPTH and (i + 1) % 2 == 1) else []
        prev_stage = make_ffn(i)(nxt_prep)
    final_prep()
    pipeline([(prev_stage, 0), (final_b, 1)], ORDER)

    finals = ["out_hist", "out_np", "out_ns", "out_v", "out_y0", "out_y1"]
    stats = P.emit(nc, st, finals)
    st.close()
    return nc, stats


def make_cst():
    c = np.zeros((128, NCST), np.float32)
    c[:, 0:128] = np.eye(128, dtype=np.float32)
    s_ = np.arange(128)[:, None]
    t_ = np.arange(128)[None, :]
    c[:, 128:256] = (s_ <= t_).astype(np.float32)
    for wi, w in enumerate(WINDOWS):
        for t in range(16):
            c[:, 256 + wi * 16 + t] = 1.0 / min(t + 1, w)
        for r in range(120):
            bl, tt = divmod(r, PB)
            if tt >= 16 - w:
                c[r, 320 + wi * 8 + bl] = 1.0
    c[0, 352:480] = 1.0
    return c


def make_pmat():
    m = np.zeros((128, 12 * 128), np.float32)
    s_ = np.arange(128)[:, None]
    t_ = np.arange(128)[None, :]
    for wi, w in enumerate(WINDOWS):
        pa = ((s_ <= t_) & (s_ > t_ - w)).astype(np.float32) / w - (s_ == t_).astype(np.float32)
        pb = (s_ >= 129 + t_ - w).astype(np.float32) / w
        n_t = np.minimum(t_ + 1, w).astype(np.float32)
        pa0 = ((s_ <= t_) & (s_ > t_ - w)).astype(np.float32) / n_t - (s_ == t_).astype(np.float32)
        m[:, (0 + wi) * 128:(0 + wi + 1) * 128] = pa
        m[:, (4 + wi) * 128:(4 + wi + 1) * 128] = pb
        m[:, (8 + wi) * 128:(8 + wi + 1) * 128] = pa0
    return m


_WNAMES = ("w_ada", "b_ada", "norm_mix", "norm_ffn", "norm_final", "pool_w_grp", "pool_scale", "pool_w_out",
           "gmlp_w_in", "gmlp_ln_g", "gmlp_ln_b", "gmlp_w_s", "gmlp_b_s", "gmlp_w_out",
           "ffn_w_gate", "ffn_w_up", "ffn_w_down")

_CACHE = {}


def run(inputs, cfg, ncores):
    key = (cfg.SEQ, cfg.DEPTH)
    if key not in _CACHE:
        _CACHE[key] = build_program(cfg)
    nc, _ = _CACHE[key]
    f = lambda a: np.ascontiguousarray(np.asarray(a, dtype=np.float32))
    wts = {n: f(inputs[n]) for n in _WNAMES}
    cstv = make_cst()
    pmatv = make_pmat()
    xpr, xsm, spl = f(inputs["x_prompt"]), f(inputs["x_sample"]), f(inputs["state_pool"])
    cp, cs = f(inputs["c_prompt"]), f(inputs["c_sample"])
    in_maps = []
    for r in range(ncores):
        m = dict(wts)
        m["xp"] = xpr[r]
        m["xs"] = np.ascontiguousarray(xsm[NS * r:NS * (r + 1), 0, :])
        m["sp"] = np.ascontiguousarray(spl[:, NS * r:NS * (r + 1)])
        m["cc"] = np.ascontiguousarray(np.concatenate([cp[r:r + 1], cs[NS * r:NS * (r + 1)]], axis=0))
        m["cst"] = cstv
        m["pmat"] = pmatv
        in_maps.append(m)
    res = run_bass_kernel_spmd(nc, in_maps, core_ids=list(range(ncores)))
    rs = res.results
    y_prompt = np.stack([rs[r]["yp"] for r in range(ncores)], axis=0)
    y_sample = np.concatenate([rs[r]["ys"] for r in range(ncores)], axis=0)[:, None, :]
    npp = np.stack([rs[r]["npp"] for r in range(ncores)], axis=1)
    nps = np.concatenate([rs[r]["nps"] for r in range(ncores)], axis=1)
    ncv = np.concatenate([rs[r]["ncv"] for r in range(ncores)], axis=1)[:, :, None, :]
    return (y_prompt.astype(np.float32), y_sample.astype(np.float32), npp.astype(np.float32),
            nps.astype(np.float32), ncv.astype(np.float32))


def kernel(**inputs):
    return run(inputs, Cfg(2048, 4), NCORES)
```
